# Optimizing a Trainium2 kernel written in Bass

```python
import jax, jax.numpy as jnp
from jax import lax
import numpy as np

D_MODEL = 1024
BATCH = 2
SEQ = 8192
DEPTH = 2

GRID_W = 64
CTX_LEN = 256
N_MIXERS = 2
RET_HEADS = 4
RET_QK_DIM = D_MODEL // RET_HEADS
RET_V_DIM = 2 * D_MODEL // RET_HEADS
RET_CHUNK = 128
DECAY_EXP_FWD = 5.0
DECAY_EXP_BWD = 5.5
ROPE_BASE = 10000.0
CONV_WIDTH = 3
FFN_HIDDEN = -(-8 * D_MODEL // (3 * 256)) * 256
N_RET_LAYERS = (DEPTH + N_MIXERS - 1) // N_MIXERS
N_CONV_LAYERS = DEPTH // N_MIXERS
EPS = 1e-6

kernel_name = "hybrid_retention_shortconv_dit"


def rms_norm(x, gain=None):
    xf = x.astype(jnp.float32)
    y = xf * lax.rsqrt(jnp.mean(xf * xf, axis=-1, keepdims=True) + EPS)
    if gain is not None:
        y = y * gain.astype(jnp.float32)
    return y.astype(x.dtype)


def ada_params(cond, w, b):
    return jnp.split(jax.nn.silu(cond) @ w + b, 6, axis=-1)


def modulate(x, shift, scale):
    return x * (1 + scale) + shift


def rope_1d(x, pos):
    n = x.shape[-1] // 2
    freqs = ROPE_BASE ** (-jnp.arange(n, dtype=jnp.float32) / n)
    ang = pos.astype(jnp.float32)[:, None] * freqs[None, :]
    cos, sin = jnp.cos(ang).astype(x.dtype), jnp.sin(ang).astype(x.dtype)
    x1, x2 = x[..., :n], x[..., n:]
    return jnp.concatenate([x1 * cos - x2 * sin, x1 * sin + x2 * cos], axis=-1)


def rope_2d(x, row, col):
    half = x.shape[-1] // 2
    return jnp.concatenate([rope_1d(x[..., :half], row), rope_1d(x[..., half:], col)], axis=-1)


def retention_log_decays():
    h = jnp.arange(RET_HEADS, dtype=jnp.float32)
    lg_f = jnp.log1p(-jnp.exp2(-DECAY_EXP_FWD - h))
    lg_b = jnp.log1p(-jnp.exp2(-DECAY_EXP_BWD - h))
    return lg_f, lg_b


def retention_chunk_scan(q, k, v, log_g, s0):
    bsz, nh, L, _ = q.shape
    dv = v.shape[-1]
    n_chunks = L // RET_CHUNK

    def to_chunks(t):
        return t.reshape(bsz, nh, n_chunks, RET_CHUNK, t.shape[-1]).transpose(2, 0, 1, 3, 4)

    idx = jnp.arange(RET_CHUNK, dtype=jnp.float32)
    diff = idx[:, None] - idx[None, :]
    decay_mat = jnp.where(diff[None] >= 0,
                          jnp.exp(jnp.maximum(diff, 0.0)[None] * log_g[:, None, None]), 0.0)
    q_dec = jnp.exp((idx[None, :] + 1.0) * log_g[:, None])[None, :, :, None]
    k_dec = jnp.exp((RET_CHUNK - 1.0 - idx[None, :]) * log_g[:, None])[None, :, :, None]
    chunk_dec = jnp.exp(RET_CHUNK * log_g)[None, :, None, None]

    def step(s, inp):
        qc, kc, vc = inp
        scores = jnp.einsum('bhid,bhjd->bhij', qc, kc) * decay_mat[None]
        intra = jnp.einsum('bhij,bhje->bhie', scores, vc)
        cross = jnp.einsum('bhid,bhde->bhie', qc * q_dec, s)
        s_new = s * chunk_dec + jnp.einsum('bhjd,bhje->bhde', kc * k_dec, vc)
        return s_new, intra + cross

    _, out = lax.scan(step, s0, (to_chunks(q), to_chunks(k), to_chunks(v)))
    return out.transpose(1, 2, 0, 3, 4).reshape(bsz, nh, L, dv)


def context_state(k, v, log_g, reverse):
    L = k.shape[2]
    j = jnp.arange(L, dtype=jnp.float32)
    dist = j if reverse else (L - 1.0) - j
    w = jnp.exp(dist[None, :] * log_g[:, None])
    return jnp.einsum('bhld,bhle->bhde', k * w[None, :, :, None], v)


def context_parallel(q, k, v, lg_f, lg_b):
    L = q.shape[2]
    idx = jnp.arange(L, dtype=jnp.float32)
    diff = idx[:, None] - idx[None, :]
    d_f = jnp.where(diff[None] >= 0, jnp.exp(jnp.maximum(diff, 0.0)[None] * lg_f[:, None, None]), 0.0)
    d_b = jnp.where(diff[None] <= 0, jnp.exp(jnp.maximum(-diff, 0.0)[None] * lg_b[:, None, None]), 0.0)
    scores = jnp.einsum('bhid,bhjd->bhij', q, k) * (d_f + d_b)[None]
    return jnp.einsum('bhij,bhje->bhie', scores, v)


def retention_output(o, g, w_o):
    bsz, nh, L, dv = o.shape
    o = o * lax.rsqrt(jnp.mean(o * o, axis=-1, keepdims=True) + EPS)
    o = o.transpose(0, 2, 1, 3).reshape(bsz, L, nh * dv).astype(g.dtype)
    return (jax.nn.silu(g) * o) @ w_o


def retention_mixer(ax, ac, w_qkvg, w_o, row, col, with_ctx_out):
    D = D_MODEL
    scale = RET_QK_DIM ** -0.5

    def heads(t, d):
        return t.reshape(t.shape[0], t.shape[1], RET_HEADS, d).transpose(0, 2, 1, 3).astype(jnp.float32)

    qx, kx, vx, gx = jnp.split(ax @ w_qkvg, [D, 2 * D, 4 * D], axis=-1)
    qx = rope_2d(heads(qx, RET_QK_DIM), row, col)
    kx = rope_2d(heads(kx, RET_QK_DIM), row, col) * scale
    vx = heads(vx, RET_V_DIM)

    if with_ctx_out:
        qc, kc, vc, gc = jnp.split(ac @ w_qkvg, [D, 2 * D, 4 * D], axis=-1)
    else:
        kc, vc = jnp.split(ac @ w_qkvg[:, D:4 * D], [D], axis=-1)
    kc = heads(kc, RET_QK_DIM) * scale
    vc = heads(vc, RET_V_DIM)

    lg_f, lg_b = retention_log_decays()
    s_f = context_state(kc, vc, lg_f, reverse=False)
    s_b = context_state(kc, vc, lg_b, reverse=True)

    o_f = retention_chunk_scan(qx, kx, vx, lg_f, s_f)
    o_b = jnp.flip(retention_chunk_scan(jnp.flip(qx, 2), jnp.flip(kx, 2), jnp.flip(vx, 2), lg_b, s_b), 2)
    yx = retention_output(o_f + o_b, gx, w_o)

    yc = None
    if with_ctx_out:
        yc = retention_output(context_parallel(heads(qc, RET_QK_DIM), kc, vc, lg_f, lg_b), gc, w_o)
    return yx, yc


def short_conv(u, w):
    return lax.conv_general_dilated(
        u, w[:, None, :].astype(u.dtype), window_strides=(1,),
        padding=[(CONV_WIDTH // 2, CONV_WIDTH // 2)],
        dimension_numbers=('NWC', 'WIO', 'NWC'), feature_group_count=u.shape[-1])


def conv_mixer(h, w_in, w_conv, w_out):
    b_gate, c_gate, xv = jnp.split(h @ w_in, 3, axis=-1)
    return (b_gate * short_conv(c_gate * xv, w_conv)) @ w_out


def swiglu(h, w1, w3, w2):
    return (jax.nn.silu(h @ w1) * (h @ w3)) @ w2


def setup_inputs(seed: int = 0) -> dict:
    key = jax.random.key(seed)
    ks = jax.random.split(key, 20)
    D, F = D_MODEL, FFN_HIDDEN
    nrm = jax.random.normal
    f32 = jnp.float32
    return {
        "x": nrm(ks[0], (BATCH, SEQ, D), f32),
        "c": nrm(ks[1], (BATCH, D), f32),
        "ctx": nrm(ks[2], (BATCH, CTX_LEN, D), f32),
        "c_ctx": nrm(ks[3], (D,), f32),
        "ada_w": nrm(ks[4], (DEPTH, D, 6 * D), f32) * (0.5 * D ** -0.5),
        "ada_b": nrm(ks[5], (DEPTH, 6 * D), f32) * 0.02,
        "norm_mix": 1.0 + 0.02 * nrm(ks[6], (DEPTH, D), f32),
        "norm_ffn": 1.0 + 0.02 * nrm(ks[7], (DEPTH, D), f32),
        "ret_w_qkvg": nrm(ks[8], (N_RET_LAYERS, D, 6 * D), f32) * D ** -0.5,
        "ret_w_o": nrm(ks[9], (N_RET_LAYERS, 2 * D, D), f32) * (2 * D) ** -0.5,
        "conv_w_in": nrm(ks[10], (N_CONV_LAYERS, D, 3 * D), f32) * D ** -0.5,
        "conv_w": nrm(ks[11], (N_CONV_LAYERS, CONV_WIDTH, D), f32) * CONV_WIDTH ** -0.5,
        "conv_w_out": nrm(ks[12], (N_CONV_LAYERS, D, D), f32) * D ** -0.5,
        "ffn_w1": nrm(ks[13], (DEPTH, D, F), f32) * D ** -0.5,
        "ffn_w3": nrm(ks[14], (DEPTH, D, F), f32) * D ** -0.5,
        "ffn_w2": nrm(ks[15], (DEPTH, F, D), f32) * F ** -0.5,
        "final_norm": 1.0 + 0.02 * nrm(ks[16], (D,), f32),
    }


def reference(x, c, ctx, c_ctx, ada_w, ada_b, norm_mix, norm_ffn, ret_w_qkvg, ret_w_o,
              conv_w_in, conv_w, conv_w_out, ffn_w1, ffn_w3, ffn_w2, final_norm):
    T = x.shape[1]
    rows = T // GRID_W
    row = jnp.repeat(jnp.arange(rows, dtype=jnp.int32), GRID_W)
    col = jnp.tile(jnp.arange(GRID_W, dtype=jnp.int32), rows)

    hx, hc = x, ctx
    for i in range(DEPTH):
        mixer = i % N_MIXERS
        ctx_needed = any(j % N_MIXERS == 0 for j in range(i + 1, DEPTH))
        sh1, sc1, g1, sh2, sc2, g2 = [t[:, None, :] for t in ada_params(c, ada_w[i], ada_b[i])]
        ax = modulate(rms_norm(hx, norm_mix[i]), sh1, sc1)
        if mixer == 0 or ctx_needed:
            csh1, csc1, cg1, csh2, csc2, cg2 = ada_params(c_ctx, ada_w[i], ada_b[i])
            ac = modulate(rms_norm(hc, norm_mix[i]), csh1, csc1)

        if mixer == 0:
            r = i // N_MIXERS
            yx, yc = retention_mixer(ax, ac, ret_w_qkvg[r], ret_w_o[r], row, col, ctx_needed)
        else:
            r = i // N_MIXERS
            yx = conv_mixer(ax, conv_w_in[r], conv_w[r], conv_w_out[r])
            yc = conv_mixer(ac, conv_w_in[r], conv_w[r], conv_w_out[r]) if ctx_needed else None

        hx = hx + g1 * yx
        fx = modulate(rms_norm(hx, norm_ffn[i]), sh2, sc2)
        hx = hx + g2 * swiglu(fx, ffn_w1[i], ffn_w3[i], ffn_w2[i])

        if ctx_needed:
            hc = hc + cg1 * yc
            fc = modulate(rms_norm(hc, norm_ffn[i]), csh2, csc2)
            hc = hc + cg2 * swiglu(fc, ffn_w1[i], ffn_w3[i], ffn_w2[i])

    return rms_norm(hx, final_norm)
```

```python
import numpy as np
import concourse.bass as bass
import concourse.mybir as mybir
from concourse.bass_utils import run_bass_kernel_spmd

F32 = mybir.dt.float32
BF16 = mybir.dt.bfloat16
AF = mybir.ActivationFunctionType
ALU = mybir.AluOpType

D = 1024
SEQ = 8192
NCORE = 8
NTOK = 2048
TT = 512
NTILE = NTOK // TT
H = 4
DK = 256
DV = 512
FF = 2816
HC = FF // 128
CTX = 256
EPS = 1e-6
GRID_W = 64


class Res:
    __slots__ = ("name", "w", "r")

    def __init__(self, name):
        self.name = name
        self.w = None
        self.r = {}


class Prog:
    COMPUTE = ("pe", "act", "dve", "pool")
    NDMA = 8
    NPOOL = 2

    def __init__(self, nc):
        self.nc = nc
        self.streams = {e: [] for e in ("pe", "act", "dve", "pool", "sp")}
        self.count = {e: 0 for e in self.COMPUTE}
        self.dma_i = {"sp": 0, "pool": 0}
        self.seen = {e: {} for e in self.streams}
        self.sems = {}
        self.cc_keys = []

    def set_sems(self, sems):
        self.sems = sems

    def _waits(self, eng, reads, writes):
        need = {}
        for r in reads:
            if r.w is not None:
                k, v = r.w
                need[k] = max(need.get(k, 0), v)
        for w in writes:
            if w.w is not None:
                k, v = w.w
                need[k] = max(need.get(k, 0), v)
            for k, v in w.r.items():
                need[k] = max(need.get(k, 0), v)
        out = []
        for k, v in need.items():
            if k == "pe" and eng == "pe":
                continue
            if self.seen[eng].get(k, 0) >= v:
                continue
            self.seen[eng][k] = v
            out.append((k, v))
        return out

    def _emit_waits(self, eng, waits):
        for k, v in waits:
            self.streams[eng].append(("wait", k, v))

    def _commit(self, ev, reads, writes):
        k, v = ev
        for w in writes:
            w.w = ev
            w.r = {}
        for r in reads:
            if r in writes:
                continue
            r.r[k] = max(r.r.get(k, 0), v)

    def op(self, eng, fn, reads=(), writes=()):
        waits = self._waits(eng, reads, writes)
        self._emit_waits(eng, waits)
        self.count[eng] += 1
        ev = (eng, self.count[eng])
        self.streams[eng].append(("op", fn, eng, 1))
        self._commit(ev, reads, writes)

    def dma(self, q, fn, reads=(), writes=()):
        i = self.dma_i[q]
        self.dma_i[q] += 1
        nd = self.NDMA if q == "sp" else self.NPOOL
        slot = i % nd
        key = "%s_d%d" % (q, slot)
        waits = self._waits(q, reads, writes)
        prev = 16 * (i // nd)
        if prev > 0 and self.seen[q].get(key, 0) < prev:
            self.seen[q][key] = prev
            waits.append((key, prev))
        self._emit_waits(q, waits)
        ev = (key, prev + 16)
        self.streams[q].append(("op", fn, key, 16))
        self._commit(ev, reads, writes)
        return ev

    def cc(self, key, fn, reads=(), writes=()):
        waits = self._waits("pool", reads, writes)
        self._emit_waits("pool", waits)
        ev = (key, 1)
        self.cc_keys.append(key)
        self.streams["pool"].append(("op", fn, key, 1))
        self._commit(ev, reads, writes)

    def barrier(self, include_pool=False):
        targets = {}
        for e in self.COMPUTE:
            if self.count[e]:
                targets[e] = self.count[e]
        for q in (("sp", "pool") if include_pool else ("sp",)):
            n = self.dma_i[q]
            nd = self.NDMA if q == "sp" else self.NPOOL
            for s in range(nd):
                cnt = (n - s + nd - 1) // nd
                if cnt > 0:
                    targets["%s_d%d" % (q, s)] = 16 * cnt
        for k in self.cc_keys:
            targets[k] = 1
        for e in self.streams:
            for k, v in targets.items():
                if self.seen[e].get(k, 0) >= v:
                    continue
                self.seen[e][k] = v
                self.streams[e].append(("wait", k, v))

    def replay(self, block):
        nc = self.nc
        sems = self.sems

        def run(stream):
            def f(eng):
                for it in stream:
                    if it[0] == "wait":
                        eng.wait_ge(sems[it[1]], it[2])
                    else:
                        _, fn, key, inc = it
                        ins = fn(eng)
                        ins.then_inc(sems[key], inc)
            return f

        block.tensor(run(self.streams["pe"]))
        block.scalar(run(self.streams["act"]))
        block.vector(run(self.streams["dve"]))
        block.gpsimd(run(self.streams["pool"]))
        block.sync(run(self.streams["sp"]))

    def handoff(self, olds, news):
        ev = {}
        for o in olds:
            if o.w is not None:
                ev[o.w[0]] = max(ev.get(o.w[0], 0), o.w[1])
            for k, v in o.r.items():
                ev[k] = max(ev.get(k, 0), v)
        for n in news:
            for k, v in ev.items():
                n.r[k] = max(n.r.get(k, 0), v)


class Arena:
    def __init__(self, t, n):
        self.t, self.n, self.off = t, n, 0

    def alloc(self, n, dt):
        nb = n * (2 if dt == BF16 else 4)
        ne = ((nb + 1) // 2 + 15) // 16 * 16
        o = self.off
        assert o + ne <= self.n, "arena overflow: need %d have %d" % (o + ne, self.n)
        self.off += ne
        v = self.t[:, o:o + ne]
        if dt == F32:
            return v.bitcast(F32)[:, 0:n]
        return v[:, 0:n]


def _mmgroup(items):
    def f(e):
        ins = None
        for (o, l, r, s, t) in items:
            ins = e.matmul(o, l, r, start=s, stop=t)
        return ins
    return f


def build_nc(debug=False, stage=9):
    from contextlib import ExitStack
    nc = bass.Bass("TRN2", target_bir_lowering=False)

    def din(name, shape, dt=F32):
        return nc.dram_tensor(name, list(shape), dt, kind="ExternalInput").ap()

    def dscr(name, shape, dt=F32):
        return nc.dram_tensor(name, list(shape), dt).ap()

    x_d = din("x", [NTOK, D])
    ctx_d = din("ctxb", [CTX, D])
    cT_d = din("cT", [128, 16])
    adaw_d = din("ada_w", [2, 12, 128, 8 * 512])
    adab_d = din("ada_b", [2, 1, 6144])
    gains_d = din("gains", [128, 32])
    fing_d = din("fin_g", [128, D])
    wqkvg_d = din("wqkvg", [12, 128, 8 * 512])
    wo_d = din("wo", [2, 2, 128, 8 * 512])
    w13_d = din("w13", [2, HC, 128, 2 * 8 * 128])
    w2_d = din("w2", [2, 2, 2, 128, 11 * 512])
    win_d = din("win", [8, 128, 3 * 8 * 128])
    wout_d = din("wout", [2, 128, 8 * 512])
    convw_d = din("convw", [128, 24])
    ropeC_d = din("ropeC", [128, 16, 256])
    ropeS_d = din("ropeS", [128, 16, 256])
    maskT_d = din("maskT", [128, 4 * 896])
    qdec_d = din("qdec", [128, 2 * 4 * 512])
    kdec_d = din("kdec", [128, 32])
    kdecx_d = din("kdecx", [128, 16])
    ccoef_d = din("ccoef", [128, 40])
    hsel_d = din("hsel", [8, 2])
    hflag_d = din("hflag", [128, 2])
    ident_d = din("ident", [128, 128])
    out_d = nc.dram_tensor("out", [NTOK, D], F32, kind="ExternalOutput").ap()
    if debug:
        dbg_d = nc.dram_tensor("dbg", [NTOK, D], F32, kind="ExternalOutput").ap()
        dbg2_d = nc.dram_tensor("dbg2", [NTOK, D], F32, kind="ExternalOutput").ap()
        dbg3_d = nc.dram_tensor("dbg3", [2, 2, 6144], F32, kind="ExternalOutput").ap()

    wqkvg_b = dscr("wqkvg_b", [12, 128, 8 * 512], BF16)
    wo_b = dscr("wo_b", [2, 2, 128, 8 * 512], BF16)
    w13_b = dscr("w13_b", [2, HC, 128, 2 * 8 * 128], BF16)
    w2_b = dscr("w2_b", [2, 2, 2, 128, 11 * 512], BF16)
    win_b = dscr("win_b", [8, 128, 3 * 8 * 128], BF16)
    wout_b = dscr("wout_b", [2, 128, 8 * 512], BF16)
    adarow_s = dscr("adarow_s", [2, 2, 6144])
    qT_s = dscr("qT_s", [NTILE, 128, 8 * TT], BF16)
    kT_s = dscr("kT_s", [NTILE, 128, 8 * TT], BF16)
    v_s = dscr("v_s", [NTILE, 128, 4 * 2048], BF16)
    sg_s = dscr("sg_s", [16, 128, 2048], BF16)
    C_s = dscr("C_s", [2, NTILE, 128, 4096])
    sctx_s = dscr("sctx_s", [2, 128, 4096])
    ag_in = [dscr("ag_in%d" % k, [128, 2048]) for k in range(4)]
    ag_out = [dscr("ag_out%d" % k, [4 * 128, 2048]) for k in range(4)]
    S_s = dscr("S_s", [2, NTILE, 128, 4096], BF16)
    hx0_s = dscr("hx0_s", [NTOK, D])
    ag2_in = dscr("ag2_in", [2, D])
    ag2_out = dscr("ag2_out", [8, D])

    lg_f = [float(np.log1p(-2.0 ** (-5.0 - h))) for h in range(H)]
    lg_b = [float(np.log1p(-2.0 ** (-5.5 - h))) for h in range(H)]
    g512f = [float(np.exp(512 * lg_f[h])) for h in range(H)]
    g512b = [float(np.exp(512 * lg_b[h])) for h in range(H)]

    es = ExitStack()
    with es:
        NAR = 106352
        arena_t = es.enter_context(nc.sbuf_tensor("arena", [128, NAR], BF16))
        banks = [es.enter_context(nc.psum_tensor("bank%d" % i, [128, 512], F32)) for i in range(8)]
        Rbank = [Res("bank%d" % i) for i in range(8)]
        P = Prog(nc)
        keys = list(P.COMPUTE) + ["sp_d%d" % i for i in range(P.NDMA)] + \
            ["pool_d%d" % i for i in range(P.NDMA)] + ["cc0", "cc1", "cc2", "cc3", "cc4"]
        P.set_sems({k: es.enter_context(nc.semaphore(k)) for k in keys})
        block = es.enter_context(nc.Block())
        A = Arena(arena_t, NAR)
        bstate = [0]

        def nb():
            i = bstate[0] % 8
            bstate[0] += 1
            return banks[i], Rbank[i]

        rr = {"evac": 0}

        def evac_copy(dst, src, reads, writes):
            rr["evac"] += 1
            if rr["evac"] % 2:
                P.op("act", lambda e: e.activation(out=dst, in_=src, func=AF.Copy), reads, writes)
            else:
                P.op("dve", lambda e: e.tensor_copy(out=dst, in_=src), reads, writes)

        def sp_dma(dst, src, reads=(), writes=()):
            P.dma("sp", lambda e: e.dma_start(out=dst, in_=src), reads, writes)

        def pool_dma(dst, src, reads=(), writes=()):
            P.dma("pool", lambda e: e.dma_start(out=dst, in_=src), reads, writes)

        Rwq = [Res("wq%d" % i) for i in range(12)]
        Rwo_c = [[Res("wo%d%d" % (a, b_)) for b_ in range(2)] for a in range(2)]
        Rw13_c = [[Res("w13_%d_%d" % (l, hc)) for hc in range(HC)] for l in range(2)]
        Rw2_c = [[[[Res("w2_%d%d%d%d" % (l, a, b_, c_)) for c_ in range(2)] for b_ in range(2)] for a in range(2)]
                 for l in range(2)]
        Rwin_c = [Res("win%d" % i) for i in range(8)]
        Rwout_c = [Res("wout%d" % i) for i in range(2)]

        def conv_ffn(l):
            for hc in range(HC):
                pool_dma(w13_b[l, hc], w13_d[l, hc], writes=[Rw13_c[l][hc]])
            for a in range(2):
                for b_ in range(2):
                    pool_dma(w2_b[l, a, b_][:, 0:3072], w2_d[l, a, b_][:, 0:3072], writes=[Rw2_c[l][a][b_][0]])
                    pool_dma(w2_b[l, a, b_][:, 3072:5632], w2_d[l, a, b_][:, 3072:5632], writes=[Rw2_c[l][a][b_][1]])

        for blk in [2, 3, 4, 5, 6, 7, 0, 1, 8, 9, 10, 11]:
            pool_dma(wqkvg_b[blk], wqkvg_d[blk], writes=[Rwq[blk]])
        for a in range(2):
            for b_ in range(2):
                pool_dma(wo_b[a, b_], wo_d[a, b_], writes=[Rwo_c[a][b_]])
        conv_ffn(0)

        def conv_layer1():
            for fc in range(8):
                pool_dma(win_b[fc], win_d[fc], writes=[Rwin_c[fc]])
            for a in range(2):
                pool_dma(wout_b[a], wout_d[a], writes=[Rwout_c[a]])
            conv_ffn(1)

        ident_f = A.alloc(128, F32); ident_b = A.alloc(128, BF16); ones_f = A.alloc(128, F32)
        epsc = A.alloc(16, F32)
        mods = A.alloc(64, F32)
        modc = A.alloc(16, F32)
        gains = A.alloc(32, F32)
        scT = A.alloc(16, F32)
        kdec = A.alloc(32, F32); kdecx = A.alloc(16, F32); ccoef = A.alloc(40, F32)
        hflag = A.alloc(16, F32); convw = A.alloc(24, F32)
        ssq = A.alloc(16, F32); std = A.alloc(16, F32); rstd = A.alloc(16, F32)
        ssqo = A.alloc(16, F32); stdo = A.alloc(16, F32); rstdo = A.alloc(16, F32)
        junk = A.alloc(1024, BF16)
        Rconst = Res("const"); Rmods = Res("mods"); Rstat = Res("stat"); Rstato = Res("stato"); Rjunk = Res("junk")
        for dst, src in ((ident_f, ident_d), (gains, gains_d), (scT, cT_d), (kdec, kdec_d), (kdecx, kdecx_d),
                         (ccoef, ccoef_d), (hflag[:, 0:2], hflag_d), (convw, convw_d)):
            sp_dma(dst, src, writes=[Rconst])
        P.op("dve", lambda e: e.memset(ones_f, 1.0), writes=[Rconst])
        P.op("dve", lambda e: e.memset(epsc, EPS), writes=[Rconst])
        P.op("act", lambda e: e.activation(out=ident_b, in_=ident_f, func=AF.Copy), reads=[Rconst], writes=[Rconst])
        P.op("act", lambda e: e.activation(out=scT, in_=scT, func=AF.Silu), reads=[Rconst], writes=[Rconst])
        pmark = A.off

        def modAB(l, s):
            o = (l * 2 + s) * 16
            return mods[:, o:o + 8], mods[:, o + 8:o + 16]

        def norm_tile(src_views, Rsrc, rows, xs_v, Rxs, dstT, RdstT, mA, mB, stat=(ssq, std, rstd, Rstat)):
            sq, sd, rs, Rs = stat
            ntc = len(src_views)
            for c in range(ntc):
                P.op("act", (lambda c: lambda e: e.activation(out=junk[0:rows, :], in_=src_views[c], func=AF.Square,
                                                              accum_out=sq[0:rows, c:c + 1]))(c),
                     reads=[Rsrc], writes=[Rs, Rjunk])
            P.op("act", lambda e: e.activation(out=sd[0:rows, 0:ntc], in_=sq[0:rows, 0:ntc], func=AF.Sqrt,
                                               scale=1.0 / D, bias=epsc[0:rows, 0:1]), reads=[Rs, Rconst], writes=[Rs])
            P.op("dve", lambda e: e.reciprocal(out=rs[0:rows, 0:ntc], in_=sd[0:rows, 0:ntc]), reads=[Rs], writes=[Rs])
            for c in range(ntc):
                P.op("act", (lambda c: lambda e: e.activation(out=xs_v[c][0:rows, :], in_=src_views[c], func=AF.Identity,
                                                              scale=rs[0:rows, c:c + 1]))(c),
                     reads=[Rsrc, Rs], writes=[Rxs])
            ncols = rows if ntc == 1 and rows < 128 else ntc * 128
            for kc in range(8):
                bk, Rb = nb()
                items = []
                for c in range(ntc):
                    items.append((bk[:, c * 128:c * 128 + rows], xs_v[c][0:rows, kc * 128:(kc + 1) * 128],
                                  ident_b[0:rows, 0:rows], True, True))
                P.op("pe", _mmgroup(items), reads=[Rxs, Rconst], writes=[Rb])
                P.op("act", (lambda kc, bk: lambda e: e.activation(out=dstT[:, kc, 0:ncols], in_=bk[:, 0:ncols],
                                                                   func=AF.Identity, scale=mA[:, kc:kc + 1],
                                                                   bias=mB[:, kc:kc + 1]))(kc, bk),
                     reads=[Rb, Rmods], writes=[RdstT])

        aring = [A.alloc(4096, F32) for _ in range(2)]
        bring = [A.alloc(512, F32) for _ in range(2)]
        arows = A.alloc(6144, F32)
        adaT = A.alloc(96, F32)
        Raring = [Res("aring%d" % i) for i in range(2)]
        Rbring = [Res("bring%d" % i) for i in range(2)]
        Rarows = Res("arows"); RadaT = Res("adaT"); Radarow_s = [Res("adarow_s%d" % l) for l in range(2)]
        scT3 = scT.rearrange("p (k c) -> p k c", c=2)
        for l in range(2):
            for blk in range(12):
                i = blk % 2
                sp_dma(aring[i], adaw_d[l, blk], writes=[Raring[i]])
                sp_dma(bring[i][0:1, :], adab_d[l, :, blk * 512:(blk + 1) * 512], writes=[Rbring[i]])
                bk, Rb = nb()
                ar3 = aring[i].rearrange("p (k n) -> p k n", k=8)
                items = [(bk[0:2, :], scT3[:, kc, :], ar3[:, kc, :], kc == 0, False) for kc in range(8)]
                items.append((bk[0:2, :], ones_f[0:1, 0:2], bring[i][0:1, :], False, True))
                P.op("pe", _mmgroup(items), reads=[Raring[i], Rbring[i], Rconst], writes=[Rb])
                P.op("dve", (lambda bk, blk: lambda e: e.tensor_copy(out=arows[0:2, blk * 512:(blk + 1) * 512],
                                                                     in_=bk[0:2, :]))(bk, blk),
                     reads=[Rb], writes=[Rarows])
            sp_dma(adarow_s[l], arows[0:2, :], reads=[Rarows], writes=[Radarow_s[l]])
            if debug:
                sp_dma(dbg3_d[l], arows[0:2, :], reads=[Rarows])
            bk, Rb = nb()
            items = [(bk[:, 2 * j:2 * j + 2], arows[0:2, j * 128:(j + 1) * 128], ident_f[0:2, 0:2], True, True)
                     for j in range(48)]
            P.op("pe", _mmgroup(items), reads=[Rarows, Rconst], writes=[Rb])
            P.op("act", (lambda bk: lambda e: e.activation(out=adaT, in_=bk[:, 0:96], func=AF.Copy))(bk),
                 reads=[Rb], writes=[RadaT])
            adaT3 = adaT.rearrange("p (j c) -> p j c", c=2)
            g3 = gains.rearrange("p (l s k) -> p l s k", l=2, s=2)
            for s in range(2):
                mA, mB = modAB(l, s)
                sci, shi = 1 + 3 * s, 3 * s
                P.op("dve", (lambda mA, sci, gv: lambda e: e.scalar_tensor_tensor(
                    out=mA, in0=adaT3[:, sci * 8:(sci + 1) * 8, 0], scalar=1.0, in1=gv,
                    op0=ALU.add, op1=ALU.mult))(mA, sci, g3[:, l, s, :]), reads=[RadaT, Rconst], writes=[Rmods])
                P.op("dve", (lambda mB, shi: lambda e: e.tensor_copy(out=mB, in_=adaT3[:, shi * 8:(shi + 1) * 8, 0]))(mB, shi),
                     reads=[RadaT], writes=[Rmods])
            if l == 0:
                P.op("dve", lambda e: e.scalar_tensor_tensor(
                    out=modc[:, 0:8], in0=adaT3[:, 8:16, 1], scalar=1.0, in1=g3[:, 0, 0, :],
                    op0=ALU.add, op1=ALU.mult), reads=[RadaT, Rconst], writes=[Rmods])
                P.op("dve", lambda e: e.tensor_copy(out=modc[:, 8:16], in_=adaT3[:, 0:8, 1]),
                     reads=[RadaT], writes=[Rmods])
        P.barrier()
        A.off = pmark
        if stage == 0:
            P.barrier(include_pool=True)
            P.replay(block)
            return nc

        def build_G(l, G1, G2, grow, RG, Rgrow):
            for s, G in ((0, G1), (1, G2)):
                base = (2 + 3 * s) * 1024
                for hf in range(2):
                    sp_dma(grow[0:1, :], adarow_s[l, 0:1, base + hf * 512:base + (hf + 1) * 512],
                           reads=[Radarow_s[l]], writes=[Rgrow])
                    bk, Rb = nb()
                    P.op("pe", (lambda bk: lambda e: e.matmul(bk[:, :], ones_f[0:1, 0:128], grow[0:1, :],
                                                              start=True, stop=True))(bk),
                         reads=[Rgrow, Rconst], writes=[Rb])
                    P.op("act", (lambda bk, G, hf: lambda e: e.activation(out=G[:, hf * 512:(hf + 1) * 512], in_=bk[:, :],
                                                                          func=AF.Copy))(bk, G, hf),
                         reads=[Rb], writes=[RG])

        xt = A.alloc(4096, F32); Rxt = Res("xt")
        xt3 = xt.rearrange("p (c d) -> p c d", c=4)
        rc = A.alloc(1024, F32); rs_ = A.alloc(1024, F32); Rrope = Res("rope")
        rc3 = rc.rearrange("p (c d) -> p c d", c=4); rs3 = rs_.rearrange("p (c d) -> p c d", c=4)
        st0 = [A.alloc(512, F32) for _ in range(2)]; st1 = [A.alloc(512, F32) for _ in range(2)]
        st2 = [A.alloc(512, F32) for _ in range(2)]
        Rst0 = [Res("st0_%d" % i) for i in range(2)]; Rst1 = [Res("st1_%d" % i) for i in range(2)]
        Rst2 = [Res("st2_%d" % i) for i in range(2)]
        Lf = A.alloc(4096, F32); Lb = A.alloc(4096, F32); RL = [Res("Lf"), Res("Lb")]
        cst = [A.alloc(512, F32) for _ in range(4)]; Rcst = [Res("cst%d" % i) for i in range(4)]
        xs = A.alloc(4096, BF16); Rxs = Res("xs")
        xs_v = [xs[:, c * 1024:(c + 1) * 1024] for c in range(4)]
        axT = A.alloc(4096, BF16); RaxT = Res("axT"); axT3 = axT.rearrange("p (k t) -> p k t", k=8)
        wring = [A.alloc(4096, BF16) for _ in range(2)]; Rwring = [Res("wring%d" % i) for i in range(2)]
        qr = A.alloc(4096, BF16); kr = A.alloc(4096, BF16); Rqr = Res("qr"); Rkr = Res("kr")
        qr3 = qr.rearrange("p (c d) -> p c d", c=4); kr3 = kr.rearrange("p (c d) -> p c d", c=4)
        ktf = A.alloc(4096, BF16); ktb = A.alloc(4096, BF16); Rkt = [Res("ktf"), Res("ktb")]
        kt3 = [ktf.rearrange("p (c d) -> p c d", c=4), ktb.rearrange("p (c d) -> p c d", c=4)]
        vv = A.alloc(8192, BF16); Rv = Res("v"); v3 = vv.rearrange("p (c d) -> p c d", c=4)
        sgst = [A.alloc(512, BF16) for _ in range(4)]; Rsgst = [Res("sgst%d" % i) for i in range(4)]
        qT = A.alloc(4096, BF16); kT = A.alloc(4096, BF16); RqT = Res("qT"); RkT = Res("kT")
        qT3 = qT.rearrange("p (k t) -> p k t", k=8); kT3 = kT.rearrange("p (k t) -> p k t", k=8)
        RqT_s = [Res("qT_s%d" % i) for i in range(NTILE)]; RkT_s = [Res("kT_s%d" % i) for i in range(NTILE)]
        Rv_s = [Res("v_s%d" % i) for i in range(NTILE)]; Rsg_s = [Res("sg_s%d" % i) for i in range(NTILE)]
        RC_s = [[Res("C_s%d_%d" % (d_, i)) for i in range(NTILE)] for d_ in range(2)]
        Rsctx_s = [Res("sctx_s0"), Res("sctx_s1")]
        Rag_in = [Res("ag_in%d" % k) for k in range(4)]; Rag_out = [Res("ag_out%d" % k) for k in range(4)]
        kdec4 = kdec.rearrange("p (d j h) -> p d j h", d=2, j=4)
        kdecx4 = kdecx.rearrange("p (d j h) -> p d j h", d=2, j=2)
        Lv = [Lf, Lb]
        P.op("dve", lambda e: e.memset(Lf, 0.0), writes=[RL[0]])
        P.op("dve", lambda e: e.memset(Lb, 0.0), writes=[RL[1]])
        P.op("dve", lambda e: e.memset(ssq, 0.0), writes=[Rstat])
        P.op("dve", lambda e: e.memset(ssqo, 0.0), writes=[Rstato])
        cnt = {"w": 0, "st": 0, "sg": 0, "cst": 0}

        def phaseA_tile(sc, is_ctx):
            ntc = 2 if is_ctx else 4
            ncols = ntc * 128
            if is_ctx:
                sp_dma(xt3[:, 0:2, :], ctx_d.rearrange("(c p) d -> p c d", p=128), writes=[Rxt])
                mA, mB = modc[:, 0:8], modc[:, 8:16]
                kd = kdecx4
            else:
                sp_dma(xt3, x_d[sc * TT:(sc + 1) * TT, :].rearrange("(c p) d -> p c d", p=128), writes=[Rxt])
                sp_dma(rc3, ropeC_d[:, sc * 4:(sc + 1) * 4, :], writes=[Rrope])
                sp_dma(rs3, ropeS_d[:, sc * 4:(sc + 1) * 4, :], writes=[Rrope])
                mA, mB = modAB(0, 0)
                kd = kdec4
            norm_tile([xt3[:, c, :] for c in range(ntc)], Rxt, 128, xs_v, Rxs, axT3, RaxT, mA, mB)
            blocks = [2, 3, 4, 5, 6, 7] if is_ctx else [2, 3, 4, 5, 6, 7, 0, 1, 8, 9, 10, 11]

            def load_w(blk):
                i = cnt["w"] % 2
                cnt["w"] += 1
                sp_dma(wring[i], wqkvg_b[blk], reads=[Rwq[blk]], writes=[Rwring[i]])
                return i

            def transposes(src3, Rsrc, dst3, Rdst):
                for g8 in range(8):
                    bk, Rb = nb()
                    items = [(bk[:, c * 128:(c + 1) * 128], src3[:, c, g8 * 128:(g8 + 1) * 128], ident_b, True, True)
                             for c in range(ntc)]
                    P.op("pe", _mmgroup(items), reads=[Rsrc, Rconst], writes=[Rb])
                    evac_copy(dst3[:, g8, 0:ncols], bk[:, 0:ncols], [Rb], [Rdst])

            nxt = load_w(blocks[0])
            for bi, blk in enumerate(blocks):
                wi = nxt
                if bi + 1 < len(blocks):
                    nxt = load_w(blocks[bi + 1])
                w3 = wring[wi].rearrange("p (k n) -> p k n", k=8)
                for c in range(ntc):
                    bk, Rb = nb()
                    items = [(bk[:, :], axT3[:, kc, c * 128:(c + 1) * 128], w3[:, kc, :], kc == 0, kc == 7)
                             for kc in range(8)]
                    P.op("pe", _mmgroup(items), reads=[RaxT, Rwring[wi]], writes=[Rb])
                    if blk < 4:
                        dst3, Rdst = (qr3, Rqr) if blk < 2 else (kr3, Rkr)
                        hp = blk % 2
                        dsl = dst3[:, c, hp * 512:(hp + 1) * 512]
                        if is_ctx:
                            evac_copy(dsl, bk[:, :], [Rb], [Rdst])
                        else:
                            j = cnt["st"] % 2
                            cnt["st"] += 1
                            P.op("act", (lambda bk, j: lambda e: e.activation(out=st0[j], in_=bk[:, :], func=AF.Copy))(bk, j),
                                 reads=[Rb], writes=[Rst0[j]])
                            s0h = st0[j].rearrange("p (h d) -> p h d", h=2)
                            P.op("dve", (lambda j, c, s0h: lambda e: e.tensor_tensor(
                                out=st1[j].rearrange("p (h d) -> p h d", h=2), in0=s0h,
                                in1=rc3[:, c, :].unsqueeze(1).to_broadcast([128, 2, 256]), op=ALU.mult))(j, c, s0h),
                                reads=[Rst0[j], Rrope], writes=[Rst1[j]])
                            s05 = st0[j].rearrange("p (h a two f) -> p h a two f", h=2, a=2, two=2)
                            t25 = st2[j].rearrange("p (h a two f) -> p h a two f", h=2, a=2, two=2)
                            rs4 = rs3[:, c, :].rearrange("p (a two f) -> p a two f", a=2, two=2)
                            for a_ in range(2):
                                P.op("dve", (lambda a_, s05, t25, rs4: lambda e: e.tensor_tensor(
                                    out=t25[:, :, :, a_, :], in0=s05[:, :, :, 1 - a_, :],
                                    in1=rs4[:, :, a_, :].unsqueeze(1).to_broadcast([128, 2, 2, 64]),
                                    op=ALU.mult))(a_, s05, t25, rs4),
                                    reads=[Rst0[j], Rrope], writes=[Rst2[j]])
                            P.op("dve", (lambda j, dsl: lambda e: e.tensor_tensor(out=dsl, in0=st1[j], in1=st2[j], op=ALU.add))(j, dsl),
                                 reads=[Rst1[j], Rst2[j]], writes=[Rdst])
                    elif blk < 8:
                        evac_copy(v3[:, c, (blk - 4) * 512:(blk - 3) * 512], bk[:, :], [Rb], [Rv])
                    else:
                        j = cnt["sg"] % 4
                        cnt["sg"] += 1
                        P.op("act", (lambda bk, j: lambda e: e.activation(out=sgst[j], in_=bk[:, :], func=AF.Silu))(bk, j),
                             reads=[Rb], writes=[Rsgst[j]])
                        sp_dma(sg_s[sc * 4 + c][:, (blk - 8) * 512:(blk - 7) * 512], sgst[j],
                               reads=[Rsgst[j]], writes=[Rsg_s[sc]])
                if blk == 3:
                    for c in range(ntc):
                        for d_ in range(2):
                            P.op("dve", (lambda c, d_: lambda e: e.tensor_tensor(
                                out=kt3[d_][:, c, :].rearrange("p (h d) -> p h d", h=4),
                                in0=kr3[:, c, :].rearrange("p (h d) -> p h d", h=4),
                                in1=kd[:, d_, c, :].unsqueeze(2).to_broadcast([128, 4, 256]), op=ALU.mult))(c, d_),
                                reads=[Rkr, Rconst], writes=[Rkt[d_]])
                    if not is_ctx:
                        transposes(kr3, Rkr, kT3, RkT)
                if blk == 7:
                    for d_ in range(2):
                        for h in range(H):
                            for dch in range(2):
                                bk, Rb = nb()
                                items = [(bk[:, :], kt3[d_][:, c, h * 256 + dch * 128:h * 256 + (dch + 1) * 128],
                                          v3[:, c, h * 512:(h + 1) * 512], c == 0, c == ntc - 1) for c in range(ntc)]
                                P.op("pe", _mmgroup(items), reads=[Rkt[d_], Rv], writes=[Rb])
                                j = cnt["cst"] % 4
                                cnt["cst"] += 1
                                sl = slice((h * 2 + dch) * 512, (h * 2 + dch + 1) * 512)
                                P.op("act", (lambda bk, j: lambda e: e.activation(out=cst[j], in_=bk[:, :], func=AF.Copy))(bk, j),
                                     reads=[Rb], writes=[Rcst[j]])
                                if is_ctx:
                                    sp_dma(sctx_s[d_][:, sl], cst[j], reads=[Rcst[j]], writes=[Rsctx_s[d_]])
                                else:
                                    sp_dma(C_s[d_, sc][:, sl], cst[j], reads=[Rcst[j]], writes=[RC_s[d_][sc]])
                                    if d_ == 0:
                                        P.op("dve", (lambda bk, sl, h: lambda e: e.scalar_tensor_tensor(
                                            out=Lf[:, sl], in0=Lf[:, sl], scalar=g512f[h], in1=bk[:, :],
                                            op0=ALU.mult, op1=ALU.add))(bk, sl, h), reads=[Rb, Rcst[j]], writes=[RL[0]])
                                    else:
                                        cb = float(np.exp(512.0 * sc * lg_b[h]))
                                        P.op("dve", (lambda bk, sl, cb: lambda e: e.scalar_tensor_tensor(
                                            out=Lb[:, sl], in0=bk[:, :], scalar=cb, in1=Lb[:, sl],
                                            op0=ALU.mult, op1=ALU.add))(bk, sl, cb), reads=[Rb, Rcst[j]], writes=[RL[1]])
                if blk == 1:
                    transposes(qr3, Rqr, qT3, RqT)
            if not is_ctx:
                sp_dma(qT_s[sc], qT, reads=[RqT], writes=[RqT_s[sc]])
                sp_dma(kT_s[sc], kT, reads=[RkT], writes=[RkT_s[sc]])
                sp_dma(v_s[sc], vv, reads=[Rv], writes=[Rv_s[sc]])

        phaseA_tile(0, True)
        for sc in range(NTILE):
            phaseA_tile(sc, False)
        for d_ in range(2):
            for hp in range(2):
                k = d_ * 2 + hp
                sp_dma(ag_in[k], Lv[d_][:, hp * 2048:(hp + 1) * 2048], reads=[RL[d_]], writes=[Rag_in[k]])
                P.cc("cc%d" % k, (lambda k: lambda e: e.collective_compute(
                    "AllGather", ALU.bypass, replica_groups=[[0, 1, 2, 3], [4, 5, 6, 7]],
                    ins=[ag_in[k]], outs=[ag_out[k]]))(k), reads=[Rag_in[k]], writes=[Rag_out[k]])
        P.barrier()
        conv_layer1()
        A.off = pmark
        if stage == 1:
            P.barrier(include_pool=True)
            P.replay(block)
            return nc

        pstage = [A.alloc(4096, F32) for _ in range(2)]; Rpst = [Res("pst%d" % i) for i in range(2)]
        Sin = [A.alloc(4096, F32) for _ in range(2)]; RSin = [Res("Sin0"), Res("Sin1")]
        Sbf = [A.alloc(4096, BF16) for _ in range(2)]; RSbf = [Res("Sbf0"), Res("Sbf1")]
        RS_s = [[Res("S_s%d_%d" % (d_, i)) for i in range(NTILE)] for d_ in range(2)]
        pc = {"st": 0, "bf": 0}
        for d_ in range(2):
            for term in range(5):
                j = pc["st"] % 2
                pc["st"] += 1
                if term < 4:
                    for hp in range(2):
                        k = d_ * 2 + hp
                        sp_dma(pstage[j][:, hp * 2048:(hp + 1) * 2048], ag_out[k][term * 128:(term + 1) * 128, :],
                               reads=[Rag_out[k]], writes=[Rpst[j]])
                else:
                    sp_dma(pstage[j], sctx_s[d_], reads=[Rsctx_s[d_]], writes=[Rpst[j]])
                for h in range(H):
                    ci = d_ * 20 + term * 4 + h
                    hs = slice(h * 1024, (h + 1) * 1024)
                    if term == 0:
                        P.op("dve", (lambda j, hs, ci, d_: lambda e: e.tensor_scalar_mul(
                            out=Sin[d_][:, hs], in0=pstage[j][:, hs], scalar1=ccoef[:, ci:ci + 1]))(j, hs, ci, d_),
                            reads=[Rpst[j], Rconst], writes=[RSin[d_]])
                    else:
                        P.op("dve", (lambda j, hs, ci, d_: lambda e: e.scalar_tensor_tensor(
                            out=Sin[d_][:, hs], in0=pstage[j][:, hs], scalar=ccoef[:, ci:ci + 1], in1=Sin[d_][:, hs],
                            op0=ALU.mult, op1=ALU.add))(j, hs, ci, d_),
                            reads=[Rpst[j], Rconst], writes=[RSin[d_]])
            order = list(range(NTILE)) if d_ == 0 else list(range(NTILE - 1, -1, -1))
            gg = g512f if d_ == 0 else g512b
            for n_, sc in enumerate(order):
                jb = pc["bf"] % 2
                pc["bf"] += 1
                P.op("act", (lambda jb, d_: lambda e: e.activation(out=Sbf[jb], in_=Sin[d_], func=AF.Copy))(jb, d_),
                     reads=[RSin[d_]], writes=[RSbf[jb]])
                sp_dma(S_s[d_, sc], Sbf[jb], reads=[RSbf[jb]], writes=[RS_s[d_][sc]])
                if n_ < NTILE - 1:
                    j = pc["st"] % 2
                    pc["st"] += 1
                    sp_dma(pstage[j], C_s[d_, sc], reads=[RC_s[d_][sc]], writes=[Rpst[j]])
                    for h in range(H):
                        hs = slice(h * 1024, (h + 1) * 1024)
                        P.op("dve", (lambda j, hs, h, d_, gg: lambda e: e.scalar_tensor_tensor(
                            out=Sin[d_][:, hs], in0=Sin[d_][:, hs], scalar=gg[h], in1=pstage[j][:, hs],
                            op0=ALU.mult, op1=ALU.add))(j, hs, h, d_, gg),
                            reads=[Rpst[j]], writes=[RSin[d_]])
        P.barrier()
        A.off = pmark
        if stage == 2:
            P.barrier(include_pool=True)
            P.replay(block)
            return nc

        hx = A.alloc(4096, F32); Rhx = Res("hx"); hx3 = hx.rearrange("p (c d) -> p c d", c=4)
        G1 = A.alloc(1024, F32); G2 = A.alloc(1024, F32); RG = Res("G")
        grow = A.alloc(512, F32); Rgrow = Res("grow")
        evst = [A.alloc(512, F32) for _ in range(2)]; Revst = [Res("evst0"), Res("evst1")]
        s1st = [A.alloc(512, F32) for _ in range(2)]; Rs1st = [Res("s1st0"), Res("s1st1")]
        R1 = A.alloc(11264, BF16)
        R2 = A.alloc(12288, BF16)
        R3 = A.alloc(8192, BF16)
        R4 = A.alloc(8192, BF16)
        sgb = R1[:, 0:8192].rearrange("p (c d) -> p c d", c=4); Rsgb = Res("sgb")
        hT3 = R1.rearrange("p (k t) -> p k t", k=HC); RhT = Res("hT")
        ogT3 = R2[:, 0:8192].rearrange("p (k t) -> p k t", k=16); RogT = Res("ogT")
        ogr = [R2[:, 8192 + i * 2048:8192 + (i + 1) * 2048].rearrange("p (c d) -> p c d", c=4) for i in range(2)]
        Rogr = [Res("ogr0"), Res("ogr1")]
        w2ring = [R2[:, i * 5632:(i + 1) * 5632] for i in range(2)]; Rw2ring = [Res("w2r0"), Res("w2r1")]; Rw2ringb = [Res("w2rb0"), Res("w2rb1")]
        woring = [R3[:, i * 4096:(i + 1) * 4096] for i in range(2)]; Rworing = [Res("wor0"), Res("wor1")]
        w13ring = [R3[:, i * 2048:(i + 1) * 2048] for i in range(4)]; Rw13ring = [Res("w13r%d" % i) for i in range(4)]
        qfb = [R4[:, i * 4096:(i + 1) * 4096] for i in range(2)]; Rqfb = [Res("qfT"), Res("qbT")]
        xs2 = R4[:, 0:4096]; Rxs2 = Res("xs2"); xs2_v = [xs2[:, c * 1024:(c + 1) * 1024] for c in range(4)]
        fxT = R4[:, 4096:8192]; RfxT = Res("fxT"); fxT3 = fxT.rearrange("p (k t) -> p k t", k=8)
        shared_mark = A.off
        mixer_res = [Rsgb, RogT] + Rogr + Rworing + Rqfb
        ffn_res = [RhT] + Rw2ring + Rw2ringb + Rw13ring + [Rxs2, RfxT]
        fcnt = {"w13": 0, "w2": 0, "s1": 0, "ev": 0}

        def resid_update(acc, Racc, G, half):
            for ic in range(4):
                j = fcnt["ev"] % 2
                fcnt["ev"] += 1
                P.op("dve", (lambda ic, j: lambda e: e.tensor_tensor(out=evst[j], in0=acc[ic][:, :],
                                                                     in1=G[:, half * 512:(half + 1) * 512], op=ALU.mult))(ic, j),
                     reads=[Racc[ic], RG], writes=[Revst[j]])
                P.op("dve", (lambda ic, j: lambda e: e.tensor_tensor(out=hx3[:, ic, half * 512:(half + 1) * 512],
                                                                     in0=hx3[:, ic, half * 512:(half + 1) * 512],
                                                                     in1=evst[j], op=ALU.add))(ic, j),
                     reads=[Revst[j]], writes=[Rhx])

        def ffn(l):
            P.handoff(mixer_res, ffn_res)
            mA, mB = modAB(l, 1)
            norm_tile([hx3[:, c, :] for c in range(4)], Rhx, 128, xs2_v, Rxs2, fxT3, RfxT, mA, mB)

            def load13(hc):
                i = fcnt["w13"] % 4
                fcnt["w13"] += 1
                sp_dma(w13ring[i], w13_b[l, hc], reads=[Rw13_c[l][hc]], writes=[Rw13ring[i]])
                return i
            pend = [load13(0), load13(1)]
            for hc in range(HC):
                wi = pend.pop(0)
                if hc + 2 < HC:
                    pend.append(load13(hc + 2))
                w4 = w13ring[wi].rearrange("p (j k m) -> p j k m", j=2, k=8)
                b1, Rb1 = nb()
                b3, Rb3 = nb()
                P.op("pe", _mmgroup([(b1[:, :], w4[:, 0, kc, :], fxT3[:, kc, :], kc == 0, kc == 7) for kc in range(8)]),
                     reads=[Rw13ring[wi], RfxT], writes=[Rb1])
                P.op("pe", _mmgroup([(b3[:, :], w4[:, 1, kc, :], fxT3[:, kc, :], kc == 0, kc == 7) for kc in range(8)]),
                     reads=[Rw13ring[wi], RfxT], writes=[Rb3])
                j = fcnt["s1"] % 2
                fcnt["s1"] += 1
                P.op("act", (lambda b1, j: lambda e: e.activation(out=s1st[j], in_=b1[:, :], func=AF.Silu))(b1, j),
                     reads=[Rb1], writes=[Rs1st[j]])
                P.op("dve", (lambda b3, j, hc: lambda e: e.tensor_tensor(out=hT3[:, hc, :], in0=b3[:, :], in1=s1st[j],
                                                                         op=ALU.mult))(b3, j, hc),
                     reads=[Rb3, Rs1st[j]], writes=[RhT])

            def load2(half, hcg):
                i = fcnt["w2"] % 2
                fcnt["w2"] += 1
                sp_dma(w2ring[i], w2_b[l, half, hcg], reads=Rw2_c[l][half][hcg], writes=[Rw2ring[i], Rw2ringb[i]])
                return i
            seq = [(hf, g) for hf in range(2) for g in range(2)]
            pend = [load2(*seq[0])]
            acc = Racc = None
            for n_, (half, hcg) in enumerate(seq):
                wi = pend.pop(0)
                if n_ + 1 < len(seq):
                    pend.append(load2(*seq[n_ + 1]))
                if hcg == 0:
                    pairs = [nb() for _ in range(4)]
                    acc = [p_[0] for p_ in pairs]; Racc = [p_[1] for p_ in pairs]
                w3 = w2ring[wi].rearrange("p (j n) -> p j n", j=11)
                items = []
                for j in range(11):
                    for ic in range(4):
                        items.append((acc[ic][:, :], hT3[:, hcg * 11 + j, ic * 128:(ic + 1) * 128], w3[:, j, :],
                                      hcg == 0 and j == 0, hcg == 1 and j == 10))
                P.op("pe", _mmgroup(items), reads=[RhT, Rw2ring[wi], Rw2ringb[wi]], writes=Racc)
                if hcg == 1:
                    resid_update(acc, Racc, G2, half)
            P.handoff(ffn_res, mixer_res)

        maskT = A.alloc(4 * 896, F32); qdec = A.alloc(4096, F32)
        mask3 = maskT.rearrange("p (h m) -> p h m", h=4)
        qdec4 = qdec.rearrange("p (d h t) -> p d h t", d=2, h=4)
        sp_dma(maskT, maskT_d, writes=[Rconst])
        sp_dma(qdec, qdec_d, writes=[Rconst])
        Sf = A.alloc(4096, BF16); Sb = A.alloc(4096, BF16); RSfb = [Res("Sf"), Res("Sb")]
        Sfb = [Sf, Sb]
        qT_b = A.alloc(4096, BF16); kT_b = A.alloc(4096, BF16); RqTb = Res("qTb"); RkTb = Res("kTb")
        qTb3 = qT_b.rearrange("p (k t) -> p k t", k=8); kTb3 = kT_b.rearrange("p (k t) -> p k t", k=8)
        v_b = A.alloc(8192, BF16); Rvb = Res("vb"); vb3 = v_b.rearrange("p (c d) -> p c d", c=4)
        scm = [A.alloc(2048, BF16).rearrange("p (j t) -> p j t", j=4) for _ in range(2)]
        Rscm = [Res("scm0"), Res("scm1")]
        Rhx0_s = [Res("hx0_s%d" % i) for i in range(NTILE)]
        Rag2_in = Res("ag2_in"); Rag2_out = Res("ag2_out")
        build_G(0, G1, G2, grow, RG, Rgrow)
        bc = {"wo": 0}
        for sc in range(NTILE):
            sp_dma(qT_b, qT_s[sc], reads=[RqT_s[sc]], writes=[RqTb])
            sp_dma(kT_b, kT_s[sc], reads=[RkT_s[sc]], writes=[RkTb])
            sp_dma(v_b, v_s[sc], reads=[Rv_s[sc]], writes=[Rvb])
            for d_ in range(2):
                sp_dma(Sfb[d_], S_s[d_, sc], reads=[RS_s[d_][sc]], writes=[RSfb[d_]])
            for c in range(4):
                sp_dma(sgb[:, c, :], sg_s[sc * 4 + c], reads=[Rsg_s[sc]], writes=[Rsgb])
            sp_dma(hx3, x_d[sc * TT:(sc + 1) * TT, :].rearrange("(c p) d -> p c d", p=128), writes=[Rhx])
            for d_ in range(2):
                P.op("dve", (lambda d_: lambda e: e.tensor_tensor(
                    out=qfb[d_].rearrange("p (h a t) -> p h a t", h=4, a=2),
                    in0=qT_b.rearrange("p (h a t) -> p h a t", h=4, a=2),
                    in1=qdec4[:, d_, :, :].unsqueeze(2).to_broadcast([128, 4, 2, TT]), op=ALU.mult))(d_),
                    reads=[RqTb, Rconst], writes=[Rqfb[d_]])
            qf3 = [qfb[d_].rearrange("p (k t) -> p k t", k=8) for d_ in range(2)]
            for h in range(H):
                sm = scm[h % 2]; Rsm = Rscm[h % 2]
                for jb in range(4):
                    bk, Rb = nb()
                    items = [(bk[:, :], kTb3[:, 2 * h + dch, jb * 128:(jb + 1) * 128], qTb3[:, 2 * h + dch, :],
                              dch == 0, dch == 1) for dch in range(2)]
                    P.op("pe", _mmgroup(items), reads=[RkTb, RqTb], writes=[Rb])
                    P.op("dve", (lambda bk, sm, jb, h: lambda e: e.tensor_tensor(
                        out=sm[:, jb, :], in0=bk[:, :], in1=mask3[:, h, 128 * (3 - jb):128 * (3 - jb) + 512],
                        op=ALU.mult))(bk, sm, jb, h), reads=[Rb, Rconst], writes=[Rsm])
                og = ogr[h % 2]; Rog = Rogr[h % 2]
                for ic in range(4):
                    bk, Rb = nb()
                    items = []
                    for jb in range(4):
                        items.append((bk[:, :], sm[:, jb, ic * 128:(ic + 1) * 128], vb3[:, jb, h * 512:(h + 1) * 512],
                                      jb == 0, False))
                    for d_ in range(2):
                        for dch in range(2):
                            k8 = 2 * h + dch
                            items.append((bk[:, :], qf3[d_][:, k8, ic * 128:(ic + 1) * 128],
                                          Sfb[d_][:, k8 * 512:(k8 + 1) * 512], False, d_ == 1 and dch == 1))
                    P.op("pe", _mmgroup(items), reads=[Rsm, Rvb, Rqfb[0], Rqfb[1], RSfb[0], RSfb[1]], writes=[Rb])
                    P.op("act", (lambda bk: lambda e: e.activation(out=junk[:, 0:512], in_=bk[:, :], func=AF.Square,
                                                                   accum_out=ssqo[:, 0:1]))(bk),
                         reads=[Rb], writes=[Rstato, Rjunk])
                    P.op("act", lambda e: e.activation(out=stdo[:, 0:1], in_=ssqo[:, 0:1], func=AF.Sqrt,
                                                       scale=1.0 / DV, bias=epsc[:, 0:1]),
                         reads=[Rstato, Rconst], writes=[Rstato])
                    P.op("dve", lambda e: e.reciprocal(out=rstdo[:, 0:1], in_=stdo[:, 0:1]), reads=[Rstato], writes=[Rstato])
                    P.op("dve", (lambda bk, og, ic, h: lambda e: e.scalar_tensor_tensor(
                        out=og[:, ic, :], in0=bk[:, :], scalar=rstdo[:, 0:1], in1=sgb[:, ic, h * 512:(h + 1) * 512],
                        op0=ALU.mult, op1=ALU.mult))(bk, og, ic, h),
                        reads=[Rb, Rstato, Rsgb], writes=[Rog])
                for j in range(4):
                    bk, Rb = nb()
                    items = [(bk[:, ic * 128:(ic + 1) * 128], og[:, ic, j * 128:(j + 1) * 128], ident_b, True, True)
                             for ic in range(4)]
                    P.op("pe", _mmgroup(items), reads=[Rog, Rconst], writes=[Rb])
                    evac_copy(ogT3[:, h * 4 + j, :], bk[:, :], [Rb], [RogT])

            def loadwo(half, kcg):
                i = bc["wo"] % 2
                bc["wo"] += 1
                sp_dma(woring[i], wo_b[half, kcg], reads=[Rwo_c[half][kcg]], writes=[Rworing[i]])
                return i
            seq = [(hf, g) for hf in range(2) for g in range(2)]
            pend = [loadwo(*seq[0])]
            acc = Racc = None
            for n_, (half, kcg) in enumerate(seq):
                wi = pend.pop(0)
                if n_ + 1 < len(seq):
                    pend.append(loadwo(*seq[n_ + 1]))
                if kcg == 0:
                    pairs = [nb() for _ in range(4)]
                    acc = [p_[0] for p_ in pairs]; Racc = [p_[1] for p_ in pairs]
                w3 = woring[wi].rearrange("p (k n) -> p k n", k=8)
                items = []
                for k8 in range(8):
                    for ic in range(4):
                        items.append((acc[ic][:, :], ogT3[:, kcg * 8 + k8, ic * 128:(ic + 1) * 128], w3[:, k8, :],
                                      kcg == 0 and k8 == 0, kcg == 1 and k8 == 7))
                P.op("pe", _mmgroup(items), reads=[RogT, Rworing[wi]], writes=Racc)
                if kcg == 1:
                    resid_update(acc, Racc, G1, half)
            if debug:
                sp_dma(dbg2_d[sc * TT:(sc + 1) * TT, :].rearrange("(c p) d -> p c d", p=128), hx3, reads=[Rhx])
            ffn(0)
            sp_dma(hx0_s[sc * TT:(sc + 1) * TT, :].rearrange("(c p) d -> p c d", p=128), hx3,
                   reads=[Rhx], writes=[Rhx0_s[sc]])
            if sc == 0:
                sp_dma(ag2_in[0:1, :], hx3[0:1, 0, :], reads=[Rhx], writes=[Rag2_in])
            if sc == NTILE - 1:
                sp_dma(ag2_in[1:2, :], hx3[127:128, 3, :], reads=[Rhx], writes=[Rag2_in])
        P.cc("cc4", lambda e: e.collective_compute("AllGather", ALU.bypass,
                                                   replica_groups=[[0, 1, 2, 3], [4, 5, 6, 7]],
                                                   ins=[ag2_in], outs=[ag2_out]),
             reads=[Rag2_in], writes=[Rag2_out])
        P.barrier()
        if debug:
            for sc in range(NTILE):
                sp_dma(hx3, hx0_s[sc * TT:(sc + 1) * TT, :].rearrange("(c p) d -> p c d", p=128), reads=[Rhx0_s[sc]], writes=[Rhx])
                sp_dma(dbg_d[sc * TT:(sc + 1) * TT, :].rearrange("(c p) d -> p c d", p=128), hx3, reads=[Rhx])
            P.barrier()
        A.off = shared_mark
        if stage == 3:
            P.barrier(include_pool=True)
            P.replay(block)
            return nc

        build_G(1, G1, G2, grow, RG, Rgrow)
        hrows = A.alloc(1024, F32); hsel = A.alloc(16, F32); ht = A.alloc(1024, F32)
        Rhrows = Res("hrows"); Rht = Res("ht")
        axTh = A.alloc(64, BF16); RaxTh = Res("axTh"); axTh3 = axTh.rearrange("p (k t) -> p k t", k=8)
        xsh = A.alloc(1024, BF16); Rxsh = Res("xsh")
        uh = A.alloc(64, F32); Ruh = Res("uh"); uh3 = uh.rearrange("p (f t) -> p f t", f=8)
        ulast = A.alloc(16, F32); Rulast = Res("ulast")
        cst5 = A.alloc(16, F32); Rcst5 = Res("cst5")
        fing = A.alloc(1024, F32)
        sp_dma(fing, fing_d, writes=[Rconst])
        sp_dma(hsel[0:8, 0:2], hsel_d, writes=[Rconst])
        sp_dma(hrows[0:8, :], ag2_out, reads=[Rag2_out], writes=[Rhrows])
        for hf in range(2):
            bk, Rb = nb()
            P.op("pe", (lambda bk, hf: lambda e: e.matmul(bk[0:2, :], hsel[0:8, 0:2], hrows[0:8, hf * 512:(hf + 1) * 512],
                                                          start=True, stop=True))(bk, hf),
                 reads=[Rhrows, Rconst], writes=[Rb])
            P.op("act", (lambda bk, hf: lambda e: e.activation(out=ht[0:2, hf * 512:(hf + 1) * 512], in_=bk[0:2, :],
                                                               func=AF.Copy))(bk, hf), reads=[Rb], writes=[Rht])
        for i in range(3):
            sp_dma(ht[2 + i:3 + i, :], hx0_s[(i + 1) * TT:(i + 1) * TT + 1, :], reads=[Rhx0_s[i + 1]], writes=[Rht])
        mA1, mB1 = modAB(1, 0)
        norm_tile([ht[0:5, :]], Rht, 5, [xsh], Rxsh, axTh3, RaxTh, mA1, mB1)
        winring = [R3[:, i * 3072:(i + 1) * 3072] for i in range(2)]; Rwinring = [Res("winr0"), Res("winr1")]
        woutring = [R2[:, i * 4096:(i + 1) * 4096] for i in range(2)]; Rwoutring = [Res("woutr0"), Res("woutr1")]
        zT = R1[:, 0:4096]; RzT = Res("zT"); zT3 = zT.rearrange("p (k t) -> p k t", k=8)
        axT1 = R4[:, 0:4096]; RaxT1 = Res("axT1"); axT13 = axT1.rearrange("p (k t) -> p k t", k=8)
        xs1 = R4[:, 4096:8192]; Rxs1 = Res("xs1"); xs1_v = [xs1[:, c * 1024:(c + 1) * 1024] for c in range(4)]
        cstc = [A.alloc(512, F32) for _ in range(2)]; Rcstc = [Res("cstc0"), Res("cstc1")]
        uext = [A.alloc(528, F32) for _ in range(2)]; Ruext = [Res("uext0"), Res("uext1")]
        tcv = [A.alloc(512, F32) for _ in range(2)]; Rtcv = [Res("tcv0"), Res("tcv1")]
        ost = [A.alloc(1024, F32) for _ in range(2)]; Rost = [Res("ost0"), Res("ost1")]
        conv_res = Rwinring + Rwoutring + [RzT, RaxT1, Rxs1]
        lc = {"win": 0, "u": 0, "wout": 0, "o": 0}
        cw3 = convw.rearrange("p (f k) -> p f k", f=8)

        def loadwin(fc):
            i = lc["win"] % 2
            lc["win"] += 1
            sp_dma(winring[i], win_b[fc], reads=[Rwin_c[fc]], writes=[Rwinring[i]])
            return i

        P.handoff(mixer_res + ffn_res, conv_res)
        nxt = loadwin(0)
        for fc in range(8):
            wi = nxt
            if fc + 1 < 8:
                nxt = loadwin(fc + 1)
            w4 = winring[wi].rearrange("p (t k m) -> p t k m", t=3, k=8)
            bk, Rb = nb()
            items = [(bk[:, 0:5], w4[:, 1, kc, :], axTh3[:, kc, 0:5], kc == 0, kc == 7) for kc in range(8)]
            items += [(bk[:, 8:13], w4[:, 2, kc, :], axTh3[:, kc, 0:5], kc == 0, kc == 7) for kc in range(8)]
            P.op("pe", _mmgroup(items), reads=[Rwinring[wi], RaxTh], writes=[Rb])
            P.op("act", (lambda bk: lambda e: e.activation(out=cst5[:, 0:5], in_=bk[:, 0:5], func=AF.Copy))(bk),
                 reads=[Rb], writes=[Rcst5])
            P.op("dve", (lambda bk, fc: lambda e: e.tensor_tensor(out=uh3[:, fc, 0:5], in0=bk[:, 8:13], in1=cst5[:, 0:5],
                                                                  op=ALU.mult))(bk, fc),
                 reads=[Rb, Rcst5], writes=[Ruh])
        for side in range(2):
            P.op("dve", (lambda side: lambda e: e.tensor_scalar_mul(out=uh3[:, :, side], in0=uh3[:, :, side],
                                                                    scalar1=hflag[:, side:side + 1]))(side),
                 reads=[Ruh, Rconst], writes=[Ruh])

        for sc in range(NTILE):
            sp_dma(hx3, hx0_s[sc * TT:(sc + 1) * TT, :].rearrange("(c p) d -> p c d", p=128),
                   reads=[Rhx0_s[sc]], writes=[Rhx])
            norm_tile([hx3[:, c, :] for c in range(4)], Rhx, 128, xs1_v, Rxs1, axT13, RaxT1, mA1, mB1)
            nxt = loadwin(0)
            for fc in range(8):
                wi = nxt
                if fc + 1 < 8:
                    nxt = loadwin(fc + 1)
                w4 = winring[wi].rearrange("p (t k m) -> p t k m", t=3, k=8)
                bks = [nb() for _ in range(3)]
                for t_ in range(3):
                    P.op("pe", _mmgroup([(bks[t_][0][:, :], w4[:, t_, kc, :], axT13[:, kc, :], kc == 0, kc == 7)
                                         for kc in range(8)]),
                         reads=[Rwinring[wi], RaxT1], writes=[bks[t_][1]])
                j = lc["u"] % 2
                lc["u"] += 1
                ue = uext[j]
                P.op("act", (lambda bk, j: lambda e: e.activation(out=cstc[j], in_=bk[:, :], func=AF.Copy))(bks[1][0], j),
                     reads=[bks[1][1]], writes=[Rcstc[j]])
                P.op("dve", (lambda bk, j, ue: lambda e: e.tensor_tensor(out=ue[:, 1:513], in0=bk[:, :], in1=cstc[j],
                                                                         op=ALU.mult))(bks[2][0], j, ue),
                     reads=[bks[2][1], Rcstc[j]], writes=[Ruext[j]])
                if sc == 0:
                    P.op("dve", (lambda ue, fc: lambda e: e.tensor_copy(out=ue[:, 0:1], in_=uh3[:, fc, 0:1]))(ue, fc),
                         reads=[Ruh], writes=[Ruext[j]])
                else:
                    P.op("dve", (lambda ue, fc: lambda e: e.tensor_copy(out=ue[:, 0:1], in_=ulast[:, fc:fc + 1]))(ue, fc),
                         reads=[Rulast], writes=[Ruext[j]])
                rcol = 2 + sc if sc < NTILE - 1 else 1
                P.op("dve", (lambda ue, fc, rcol: lambda e: e.tensor_copy(out=ue[:, 513:514], in_=uh3[:, fc, rcol:rcol + 1]))(ue, fc, rcol),
                     reads=[Ruh], writes=[Ruext[j]])
                P.op("dve", (lambda ue, fc: lambda e: e.tensor_copy(out=ulast[:, fc:fc + 1], in_=ue[:, 512:513]))(ue, fc),
                     reads=[Ruext[j]], writes=[Rulast])
                tv = tcv[j]
                P.op("dve", (lambda ue, tv, fc: lambda e: e.tensor_scalar_mul(out=tv, in0=ue[:, 1:513],
                                                                              scalar1=cw3[:, fc, 1:2]))(ue, tv, fc),
                     reads=[Ruext[j], Rconst], writes=[Rtcv[j]])
                P.op("dve", (lambda ue, tv, fc: lambda e: e.scalar_tensor_tensor(out=tv, in0=ue[:, 0:512], scalar=cw3[:, fc, 0:1],
                                                                                 in1=tv, op0=ALU.mult, op1=ALU.add))(ue, tv, fc),
                     reads=[Ruext[j], Rconst], writes=[Rtcv[j]])
                P.op("dve", (lambda ue, tv, fc: lambda e: e.scalar_tensor_tensor(out=tv, in0=ue[:, 2:514], scalar=cw3[:, fc, 2:3],
                                                                                 in1=tv, op0=ALU.mult, op1=ALU.add))(ue, tv, fc),
                     reads=[Ruext[j], Rconst], writes=[Rtcv[j]])
                P.op("dve", (lambda bk, tv, fc: lambda e: e.tensor_tensor(out=zT3[:, fc, :], in0=bk[:, :], in1=tv,
                                                                          op=ALU.mult))(bks[0][0], tv, fc),
                     reads=[bks[0][1], Rtcv[j]], writes=[RzT])
            for half in range(2):
                i = lc["wout"] % 2
                lc["wout"] += 1
                sp_dma(woutring[i], wout_b[half], reads=[Rwout_c[half]], writes=[Rwoutring[i]])
                pairs = [nb() for _ in range(4)]
                acc = [p_[0] for p_ in pairs]; Racc = [p_[1] for p_ in pairs]
                w3 = woutring[i].rearrange("p (k n) -> p k n", k=8)
                items = []
                for k8 in range(8):
                    for ic in range(4):
                        items.append((acc[ic][:, :], zT3[:, k8, ic * 128:(ic + 1) * 128], w3[:, k8, :], k8 == 0, k8 == 7))
                P.op("pe", _mmgroup(items), reads=[RzT, Rwoutring[i]], writes=Racc)
                resid_update(acc, Racc, G1, half)
            P.handoff(conv_res, ffn_res + mixer_res)
            ffn(1)
            P.handoff(mixer_res + ffn_res, conv_res)
            for c in range(4):
                P.op("act", (lambda c: lambda e: e.activation(out=junk, in_=hx3[:, c, :], func=AF.Square,
                                                              accum_out=ssq[:, c:c + 1]))(c),
                     reads=[Rhx], writes=[Rstat, Rjunk])
            P.op("act", lambda e: e.activation(out=std[:, 0:4], in_=ssq[:, 0:4], func=AF.Sqrt, scale=1.0 / D,
                                               bias=epsc[:, 0:1]), reads=[Rstat, Rconst], writes=[Rstat])
            P.op("dve", lambda e: e.reciprocal(out=rstd[:, 0:4], in_=std[:, 0:4]), reads=[Rstat], writes=[Rstat])
            for c in range(4):
                j = lc["o"] % 2
                lc["o"] += 1
                P.op("dve", (lambda c, j: lambda e: e.scalar_tensor_tensor(out=ost[j], in0=hx3[:, c, :], scalar=rstd[:, c:c + 1],
                                                                           in1=fing, op0=ALU.mult, op1=ALU.mult))(c, j),
                     reads=[Rhx, Rstat, Rconst], writes=[Rost[j]])
                sp_dma(out_d[sc * TT + c * 128:sc * TT + (c + 1) * 128, :], ost[j], reads=[Rost[j]])
        P.barrier(include_pool=True)
        P.replay(block)
    return nc


def _const_tables():
    lg_f = np.log1p(-np.exp2(-5.0 - np.arange(H, dtype=np.float64)))
    lg_b = np.log1p(-np.exp2(-5.5 - np.arange(H, dtype=np.float64)))
    scale = DK ** -0.5
    p = np.arange(128, dtype=np.float64)
    kdec = np.zeros((128, 2, 4, H))
    kdecx = np.zeros((128, 2, 2, H))
    for jb in range(4):
        pos = 128 * jb + p
        kdec[:, 0, jb, :] = scale * np.exp((TT - 1 - pos)[:, None] * lg_f[None, :])
        kdec[:, 1, jb, :] = scale * np.exp(pos[:, None] * lg_b[None, :])
    for jb in range(2):
        pos = 128 * jb + p
        kdecx[:, 0, jb, :] = scale * np.exp((CTX - 1 - pos)[:, None] * lg_f[None, :])
        kdecx[:, 1, jb, :] = scale * np.exp(pos[:, None] * lg_b[None, :])
    m = np.arange(896, dtype=np.float64)
    rel = m[None, :] - 384.0 - p[:, None]
    maskT = np.zeros((128, H, 896))
    for h in range(H):
        mf = np.exp(np.maximum(rel, 0.0) * lg_f[h])
        mb = np.exp(np.maximum(-rel, 0.0) * lg_b[h])
        maskT[:, h, :] = scale * np.where(rel > 0, mf, np.where(rel < 0, mb, 2.0))
    i = np.arange(TT, dtype=np.float64)
    qdec = np.zeros((128, 2, H, TT))
    for h in range(H):
        qdec[:, 0, h, :] = np.exp((i + 1.0) * lg_f[h])[None, :]
        qdec[:, 1, h, :] = np.exp((TT - i) * lg_b[h])[None, :]
    return lg_f, lg_b, kdec, kdecx, maskT, qdec


def _core_tables(j, lg_f, lg_b):
    cc = np.zeros((2, 5, H))
    for i in range(4):
        if i < j:
            cc[0, i, :] = np.exp(float(NTOK) * (j - 1 - i) * lg_f)
        if i > j:
            cc[1, i, :] = np.exp(float(NTOK) * (i - j - 1) * lg_b)
    cc[0, 4, :] = np.exp(float(NTOK) * j * lg_f)
    cc[1, 4, :] = np.exp(float(NTOK) * (3 - j) * lg_b)
    ccoef = np.broadcast_to(cc.reshape(1, 40), (128, 40)).astype(np.float32)
    ccoef = np.where(np.abs(ccoef) < 1e-37, 0.0, ccoef).astype(np.float32)
    hsel = np.zeros((8, 2), np.float32)
    hflag = np.zeros((128, 2), np.float32)
    if j > 0:
        hsel[2 * (j - 1) + 1, 0] = 1.0
        hflag[:, 0] = 1.0
    if j < 3:
        hsel[2 * (j + 1), 1] = 1.0
        hflag[:, 1] = 1.0
    t = (j * NTOK + np.arange(NTOK)).astype(np.int64)
    row = (t // GRID_W).astype(np.float32)
    col = (t % GRID_W).astype(np.float32)
    freqs = (10000.0 ** (-np.arange(64, dtype=np.float32) / 64)).astype(np.float32)
    C = np.zeros((NTOK, 256), np.float32)
    S = np.zeros((NTOK, 256), np.float32)
    for half, pos in ((0, row), (1, col)):
        ang = (pos[:, None] * freqs[None, :]).astype(np.float32)
        cs, sn = np.cos(ang).astype(np.float32), np.sin(ang).astype(np.float32)
        b0 = half * 128
        C[:, b0:b0 + 64] = cs
        C[:, b0 + 64:b0 + 128] = cs
        S[:, b0:b0 + 64] = -sn
        S[:, b0 + 64:b0 + 128] = sn
    ropeC = np.ascontiguousarray(C.reshape(16, 128, 256).transpose(1, 0, 2))
    ropeS = np.ascontiguousarray(S.reshape(16, 128, 256).transpose(1, 0, 2))
    return ccoef, hsel, hflag, ropeC, ropeS


def _prep_shared(inp):
    f = lambda a: np.ascontiguousarray(a, dtype=np.float32)
    sh = {}
    sh["ada_w"] = f(inp["ada_w"].reshape(2, 8, 128, 12, 512).transpose(0, 3, 2, 1, 4)).reshape(2, 12, 128, 4096)
    sh["ada_b"] = f(inp["ada_b"].reshape(2, 1, 6144))
    g = np.zeros((128, 2, 2, 8), np.float32)
    for l in range(2):
        g[:, l, 0, :] = inp["norm_mix"][l].reshape(8, 128).T
        g[:, l, 1, :] = inp["norm_ffn"][l].reshape(8, 128).T
    sh["gains"] = f(g.reshape(128, 32))
    sh["fin_g"] = f(np.broadcast_to(inp["final_norm"].reshape(1, D), (128, D)))
    W = inp["ret_w_qkvg"][0]
    sh["wqkvg"] = f(W.reshape(8, 128, 12, 512).transpose(2, 1, 0, 3)).reshape(12, 128, 4096)
    Wo = inp["ret_w_o"][0]
    sh["wo"] = f(Wo.reshape(2, 8, 128, 2, 512).transpose(3, 0, 2, 1, 4)).reshape(2, 2, 128, 4096)
    w13 = np.zeros((2, HC, 128, 2, 8, 128), np.float32)
    for l in range(2):
        w13[l, :, :, 0] = inp["ffn_w1"][l].reshape(8, 128, HC, 128).transpose(2, 1, 0, 3)
        w13[l, :, :, 1] = inp["ffn_w3"][l].reshape(8, 128, HC, 128).transpose(2, 1, 0, 3)
    sh["w13"] = w13.reshape(2, HC, 128, 2048)
    w2 = np.stack([inp["ffn_w2"][l].reshape(2, 11, 128, 2, 512).transpose(3, 0, 2, 1, 4) for l in range(2)])
    sh["w2"] = f(w2).reshape(2, 2, 2, 128, 11 * 512)
    Win = inp["conv_w_in"][0]
    sh["win"] = f(Win.reshape(8, 128, 3, 8, 128).transpose(3, 1, 2, 0, 4)).reshape(8, 128, 3072)
    Wout = inp["conv_w_out"][0]
    sh["wout"] = f(Wout.reshape(8, 128, 2, 512).transpose(2, 1, 0, 3)).reshape(2, 128, 4096)
    sh["convw"] = f(inp["conv_w"][0].reshape(3, 8, 128).transpose(2, 1, 0)).reshape(128, 24)
    lg_f, lg_b, kdec, kdecx, maskT, qdec = _const_tables()
    sh["kdec"] = f(kdec.reshape(128, 32))
    sh["kdecx"] = f(kdecx.reshape(128, 16))
    sh["maskT"] = f(maskT.reshape(128, 4 * 896))
    sh["qdec"] = f(qdec.reshape(128, 2 * 4 * 512))
    sh["ident"] = np.eye(128, dtype=np.float32)
    return sh, lg_f, lg_b


_NC_CACHE = {}


def kernel(**inputs):
    inp = {k: np.asarray(v) for k, v in inputs.items()}
    debug = bool(inp.pop("_debug", False))
    stage = int(inp.pop("_stage", 9))
    sh, lg_f, lg_b = _prep_shared(inp)
    in_maps = []
    for core in range(NCORE):
        b, j = core // 4, core % 4
        m = dict(sh)
        m["x"] = np.ascontiguousarray(inp["x"][b, j * NTOK:(j + 1) * NTOK, :], dtype=np.float32)
        m["ctxb"] = np.ascontiguousarray(inp["ctx"][b], dtype=np.float32)
        cT = np.zeros((128, 8, 2), np.float32)
        cT[:, :, 0] = inp["c"][b].reshape(8, 128).T
        cT[:, :, 1] = inp["c_ctx"].reshape(8, 128).T
        m["cT"] = cT.reshape(128, 16)
        ccoef, hsel, hflag, ropeC, ropeS = _core_tables(j, lg_f, lg_b)
        m["ccoef"], m["hsel"], m["hflag"], m["ropeC"], m["ropeS"] = ccoef, hsel, hflag, ropeC, ropeS
        in_maps.append(m)
    if (debug, stage) not in _NC_CACHE:
        _NC_CACHE[(debug, stage)] = build_nc(debug, stage)
    nc = _NC_CACHE[(debug, stage)]
    res = run_bass_kernel_spmd(nc, in_maps, core_ids=list(range(NCORE)))
    out = np.zeros((2, SEQ, D), np.float32)
    for core in range(NCORE):
        b, j = core // 4, core % 4
        out[b, j * NTOK:(j + 1) * NTOK, :] = res.results[core]["out"]
    if debug:
        dbg = np.zeros((2, SEQ, D), np.float32)
        dbg2 = np.zeros((2, SEQ, D), np.float32)
        for core in range(NCORE):
            b, j = core // 4, core % 4
            dbg[b, j * NTOK:(j + 1) * NTOK, :] = res.results[core]["dbg"]
            dbg2[b, j * NTOK:(j + 1) * NTOK, :] = res.results[core]["dbg2"]
        dbg3 = np.stack([res.results[core]["dbg3"] for core in range(NCORE)])
        return out, dbg, dbg2, dbg3
    return out
```

```python
import numpy as np
import concourse.bass as bass
import concourse.mybir as mybir
from concourse.bass_utils import run_bass_kernel_spmd

F32 = mybir.dt.float32
BF16 = mybir.dt.bfloat16
AF = mybir.ActivationFunctionType
ALU = mybir.AluOpType

D = 1024
SEQ = 8192
NCORE = 8
NTOK = 2048
TT = 512
NTILE = NTOK // TT
H = 4
DK = 256
DV = 512
FF = 2816
HC = FF // 128
CTX = 256
EPS = 1e-6
GRID_W = 64


class Res:
    __slots__ = ("name", "w", "r")

    def __init__(self, name):
        self.name = name
        self.w = None
        self.r = {}


class Prog:
    COMPUTE = ("pe", "act", "dve", "pool")
    NDMA = 8
    NPOOL = 2

    def __init__(self, nc):
        self.nc = nc
        self.streams = {e: [] for e in ("pe", "act", "dve", "pool", "sp")}
        self.count = {e: 0 for e in self.COMPUTE}
        self.dma_i = {"sp": 0, "pool": 0}
        self.seen = {e: {} for e in self.streams}
        self.sems = {}
        self.cc_keys = []

    def set_sems(self, sems):
        self.sems = sems

    def _waits(self, eng, reads, writes):
        need = {}
        for r in reads:
            if r.w is not None:
                k, v = r.w
                need[k] = max(need.get(k, 0), v)
        for w in writes:
            if w.w is not None:
                k, v = w.w
                need[k] = max(need.get(k, 0), v)
            for k, v in w.r.items():
                need[k] = max(need.get(k, 0), v)
        out = []
        for k, v in need.items():
            if k == "pe" and eng == "pe":
                continue
            if self.seen[eng].get(k, 0) >= v:
                continue
            self.seen[eng][k] = v
            out.append((k, v))
        return out

    def _emit_waits(self, eng, waits):
        for k, v in waits:
            self.streams[eng].append(("wait", k, v))

    def _commit(self, ev, reads, writes):
        k, v = ev
        for w in writes:
            w.w = ev
            w.r = {}
        for r in reads:
            if r in writes:
                continue
            r.r[k] = max(r.r.get(k, 0), v)

    def op(self, eng, fn, reads=(), writes=()):
        waits = self._waits(eng, reads, writes)
        self._emit_waits(eng, waits)
        self.count[eng] += 1
        ev = (eng, self.count[eng])
        self.streams[eng].append(("op", fn, eng, 1))
        self._commit(ev, reads, writes)

    def dma(self, q, fn, reads=(), writes=()):
        i = self.dma_i[q]
        self.dma_i[q] += 1
        nd = self.NDMA if q == "sp" else self.NPOOL
        slot = i % nd
        key = "%s_d%d" % (q, slot)
        waits = self._waits(q, reads, writes)
        prev = 16 * (i // nd)
        if prev > 0 and self.seen[q].get(key, 0) < prev:
            self.seen[q][key] = prev
            waits.append((key, prev))
        self._emit_waits(q, waits)
        ev = (key, prev + 16)
        self.streams[q].append(("op", fn, key, 16))
        self._commit(ev, reads, writes)
        return ev

    def cc(self, key, fn, reads=(), writes=()):
        waits = self._waits("pool", reads, writes)
        self._emit_waits("pool", waits)
        ev = (key, 1)
        self.cc_keys.append(key)
        self.streams["pool"].append(("op", fn, key, 1))
        self._commit(ev, reads, writes)

    def barrier(self):
        targets = {}
        for e in self.COMPUTE:
            if self.count[e]:
                targets[e] = self.count[e]
        for q in ("sp", "pool"):
            n = self.dma_i[q]
            nd = self.NDMA if q == "sp" else self.NPOOL
            for s in range(nd):
                cnt = (n - s + nd - 1) // nd
                if cnt > 0:
                    targets["%s_d%d" % (q, s)] = 16 * cnt
        for k in self.cc_keys:
            targets[k] = 1
        for e in self.streams:
            for k, v in targets.items():
                if self.seen[e].get(k, 0) >= v:
                    continue
                self.seen[e][k] = v
                self.streams[e].append(("wait", k, v))

    def replay(self, block):
        nc = self.nc
        sems = self.sems

        def run(stream):
            def f(eng):
                for it in stream:
                    if it[0] == "wait":
                        eng.wait_ge(sems[it[1]], it[2])
                    else:
                        _, fn, key, inc = it
                        ins = fn(eng)
                        ins.then_inc(sems[key], inc)
            return f

        block.tensor(run(self.streams["pe"]))
        block.scalar(run(self.streams["act"]))
        block.vector(run(self.streams["dve"]))
        block.gpsimd(run(self.streams["pool"]))
        block.sync(run(self.streams["sp"]))

    def handoff(self, olds, news):
        ev = {}
        for o in olds:
            if o.w is not None:
                ev[o.w[0]] = max(ev.get(o.w[0], 0), o.w[1])
            for k, v in o.r.items():
                ev[k] = max(ev.get(k, 0), v)
        for n in news:
            for k, v in ev.items():
                n.r[k] = max(n.r.get(k, 0), v)


class Arena:
    def __init__(self, t, n):
        self.t, self.n, self.off = t, n, 0

    def alloc(self, n, dt):
        nb = n * (2 if dt == BF16 else 4)
        ne = ((nb + 1) // 2 + 15) // 16 * 16
        o = self.off
        assert o + ne <= self.n, "arena overflow: need %d have %d" % (o + ne, self.n)
        self.off += ne
        v = self.t[:, o:o + ne]
        if dt == F32:
            return v.bitcast(F32)[:, 0:n]
        return v[:, 0:n]


def _mmgroup(items):
    def f(e):
        ins = None
        for (o, l, r, s, t) in items:
            ins = e.matmul(o, l, r, start=s, stop=t)
        return ins
    return f


def build_nc(debug=False, stage=9):
    from contextlib import ExitStack
    nc = bass.Bass("TRN2", target_bir_lowering=False)

    def din(name, shape, dt=F32):
        return nc.dram_tensor(name, list(shape), dt, kind="ExternalInput").ap()

    def dscr(name, shape, dt=F32):
        return nc.dram_tensor(name, list(shape), dt).ap()

    x_d = din("x", [NTOK, D])
    ctx_d = din("ctxb", [CTX, D])
    cT_d = din("cT", [128, 16])
    adaw_d = din("ada_w", [2, 12, 128, 8 * 512])
    adab_d = din("ada_b", [2, 1, 6144])
    gains_d = din("gains", [128, 32])
    fing_d = din("fin_g", [128, D])
    wqkvg_d = din("wqkvg", [12, 128, 8 * 512])
    wo_d = din("wo", [2, 2, 128, 8 * 512])
    w13_d = din("w13", [2, HC, 128, 2 * 8 * 128])
    w2_d = din("w2", [2, 2, 2, 128, 11 * 512])
    win_d = din("win", [8, 128, 3 * 8 * 128])
    wout_d = din("wout", [2, 128, 8 * 512])
    convw_d = din("convw", [128, 24])
    ropeC_d = din("ropeC", [128, 16, 256])
    ropeS_d = din("ropeS", [128, 16, 256])
    maskT_d = din("maskT", [128, 4 * 896])
    qdec_d = din("qdec", [128, 2 * 4 * 512])
    kdec_d = din("kdec", [128, 32])
    kdecx_d = din("kdecx", [128, 16])
    ccoef_d = din("ccoef", [128, 40])
    hsel_d = din("hsel", [8, 2])
    hflag_d = din("hflag", [128, 2])
    ident_d = din("ident", [128, 128])
    out_d = nc.dram_tensor("out", [NTOK, D], F32, kind="ExternalOutput").ap()
    if debug:
        dbg_d = nc.dram_tensor("dbg", [NTOK, D], F32, kind="ExternalOutput").ap()
        dbg2_d = nc.dram_tensor("dbg2", [NTOK, D], F32, kind="ExternalOutput").ap()
        dbg3_d = nc.dram_tensor("dbg3", [2, 2, 6144], F32, kind="ExternalOutput").ap()

    adarow_s = dscr("adarow_s", [2, 2, 6144])
    qT_s = dscr("qT_s", [NTILE, 128, 8 * TT], BF16)
    kT_s = dscr("kT_s", [NTILE, 128, 8 * TT], BF16)
    v_s = dscr("v_s", [NTILE, 128, 4 * 2048], BF16)
    sg_s = dscr("sg_s", [16, 128, 2048], BF16)
    C_s = dscr("C_s", [2, NTILE, 128, 4096])
    sctx_s = dscr("sctx_s", [2, 128, 4096])
    ag_in = [dscr("ag_in%d" % k, [128, 2048]) for k in range(4)]
    ag_out = [dscr("ag_out%d" % k, [4 * 128, 2048]) for k in range(4)]
    S_s = dscr("S_s", [2, NTILE, 128, 4096], BF16)
    hx0_s = dscr("hx0_s", [NTOK, D])
    ag2_in = dscr("ag2_in", [2, D])
    ag2_out = dscr("ag2_out", [8, D])

    lg_f = [float(np.log1p(-2.0 ** (-5.0 - h))) for h in range(H)]
    lg_b = [float(np.log1p(-2.0 ** (-5.5 - h))) for h in range(H)]
    g512f = [float(np.exp(512 * lg_f[h])) for h in range(H)]
    g512b = [float(np.exp(512 * lg_b[h])) for h in range(H)]

    es = ExitStack()
    with es:
        NAR = 106352
        arena_t = es.enter_context(nc.sbuf_tensor("arena", [128, NAR], BF16))
        banks = [es.enter_context(nc.psum_tensor("bank%d" % i, [128, 512], F32)) for i in range(8)]
        Rbank = [Res("bank%d" % i) for i in range(8)]
        P = Prog(nc)
        keys = list(P.COMPUTE) + ["sp_d%d" % i for i in range(P.NDMA)] + \
            ["pool_d%d" % i for i in range(P.NDMA)] + ["cc0", "cc1", "cc2", "cc3", "cc4"]
        P.set_sems({k: es.enter_context(nc.semaphore(k)) for k in keys})
        block = es.enter_context(nc.Block())
        A = Arena(arena_t, NAR)
        bstate = [0]

        def nb():
            i = bstate[0] % 8
            bstate[0] += 1
            return banks[i], Rbank[i]

        rr = {"evac": 0}

        def evac_copy(dst, src, reads, writes):
            rr["evac"] += 1
            if rr["evac"] % 2:
                P.op("act", lambda e: e.activation(out=dst, in_=src, func=AF.Copy), reads, writes)
            else:
                P.op("dve", lambda e: e.tensor_copy(out=dst, in_=src), reads, writes)

        def sp_dma(dst, src, reads=(), writes=()):
            P.dma("sp", lambda e: e.dma_start(out=dst, in_=src), reads, writes)

        def pool_dma(dst, src, reads=(), writes=()):
            P.dma("pool", lambda e: e.dma_start(out=dst, in_=src), reads, writes)

        ident_f = A.alloc(128, F32); ident_b = A.alloc(128, BF16); ones_f = A.alloc(128, F32)
        epsc = A.alloc(16, F32)
        mods = A.alloc(64, F32)
        modc = A.alloc(16, F32)
        gains = A.alloc(32, F32)
        scT = A.alloc(16, F32)
        kdec = A.alloc(32, F32); kdecx = A.alloc(16, F32); ccoef = A.alloc(40, F32)
        hflag = A.alloc(16, F32); convw = A.alloc(24, F32)
        ssq = A.alloc(16, F32); std = A.alloc(16, F32); rstd = A.alloc(16, F32)
        ssqo = A.alloc(16, F32); stdo = A.alloc(16, F32); rstdo = A.alloc(16, F32)
        junk = A.alloc(1024, BF16)
        Rconst = Res("const"); Rmods = Res("mods"); Rstat = Res("stat"); Rstato = Res("stato"); Rjunk = Res("junk")
        for dst, src in ((ident_f, ident_d), (gains, gains_d), (scT, cT_d), (kdec, kdec_d), (kdecx, kdecx_d),
                         (ccoef, ccoef_d), (hflag[:, 0:2], hflag_d), (convw, convw_d)):
            sp_dma(dst, src, writes=[Rconst])
        P.op("dve", lambda e: e.memset(ones_f, 1.0), writes=[Rconst])
        P.op("dve", lambda e: e.memset(epsc, EPS), writes=[Rconst])
        P.op("act", lambda e: e.activation(out=ident_b, in_=ident_f, func=AF.Copy), reads=[Rconst], writes=[Rconst])
        P.op("act", lambda e: e.activation(out=scT, in_=scT, func=AF.Silu), reads=[Rconst], writes=[Rconst])
        pmark = A.off

        def modAB(l, s):
            o = (l * 2 + s) * 16
            return mods[:, o:o + 8], mods[:, o + 8:o + 16]

        def norm_tile(src_views, Rsrc, rows, xs_v, Rxs, dstT, RdstT, mA, mB, stat=(ssq, std, rstd, Rstat)):
            sq, sd, rs, Rs = stat
            ntc = len(src_views)
            for c in range(ntc):
                P.op("act", (lambda c: lambda e: e.activation(out=junk[0:rows, :], in_=src_views[c], func=AF.Square,
                                                              accum_out=sq[0:rows, c:c + 1]))(c),
                     reads=[Rsrc], writes=[Rs, Rjunk])
            P.op("act", lambda e: e.activation(out=sd[0:rows, 0:ntc], in_=sq[0:rows, 0:ntc], func=AF.Sqrt,
                                               scale=1.0 / D, bias=epsc[0:rows, 0:1]), reads=[Rs, Rconst], writes=[Rs])
            P.op("dve", lambda e: e.reciprocal(out=rs[0:rows, 0:ntc], in_=sd[0:rows, 0:ntc]), reads=[Rs], writes=[Rs])
            for c in range(ntc):
                P.op("act", (lambda c: lambda e: e.activation(out=xs_v[c][0:rows, :], in_=src_views[c], func=AF.Identity,
                                                              scale=rs[0:rows, c:c + 1]))(c),
                     reads=[Rsrc, Rs], writes=[Rxs])
            ncols = rows if ntc == 1 and rows < 128 else ntc * 128
            for kc in range(8):
                bk, Rb = nb()
                items = []
                for c in range(ntc):
                    items.append((bk[:, c * 128:c * 128 + rows], xs_v[c][0:rows, kc * 128:(kc + 1) * 128],
                                  ident_b[0:rows, 0:rows], True, True))
                P.op("pe", _mmgroup(items), reads=[Rxs, Rconst], writes=[Rb])
                P.op("act", (lambda kc, bk: lambda e: e.activation(out=dstT[:, kc, 0:ncols], in_=bk[:, 0:ncols],
                                                                   func=AF.Identity, scale=mA[:, kc:kc + 1],
                                                                   bias=mB[:, kc:kc + 1]))(kc, bk),
                     reads=[Rb, Rmods], writes=[RdstT])

        Raring = [Res("aring%d" % i) for i in range(2)]
        Rbring = [Res("bring%d" % i) for i in range(2)]
        Rarows = Res("arows"); RadaT = Res("adaT"); Radarow_s = [Res("adarow_s%d" % l) for l in range(2)]
        scT3 = scT.rearrange("p (k c) -> p k c", c=2)

        def ada_layer(l):
            aring = [A.alloc(4096, F32) for _ in range(2)]
            bring = [A.alloc(512, F32) for _ in range(2)]
            arows = A.alloc(6144, F32)
            adaT = A.alloc(96, F32)
            for blk in range(12):
                i = blk % 2
                sp_dma(aring[i], adaw_d[l, blk], writes=[Raring[i]])
                sp_dma(bring[i][0:1, :], adab_d[l, :, blk * 512:(blk + 1) * 512], writes=[Rbring[i]])
                bk, Rb = nb()
                ar3 = aring[i].rearrange("p (k n) -> p k n", k=8)
                items = [(bk[0:2, :], scT3[:, kc, :], ar3[:, kc, :], kc == 0, False) for kc in range(8)]
                items.append((bk[0:2, :], ones_f[0:1, 0:2], bring[i][0:1, :], False, True))
                P.op("pe", _mmgroup(items), reads=[Raring[i], Rbring[i], Rconst], writes=[Rb])
                P.op("dve", (lambda bk, blk: lambda e: e.tensor_copy(out=arows[0:2, blk * 512:(blk + 1) * 512],
                                                                     in_=bk[0:2, :]))(bk, blk),
                     reads=[Rb], writes=[Rarows])
            sp_dma(adarow_s[l], arows[0:2, :], reads=[Rarows], writes=[Radarow_s[l]])
            if debug:
                sp_dma(dbg3_d[l], arows[0:2, :], reads=[Rarows])
            bk, Rb = nb()
            items = [(bk[:, 2 * j:2 * j + 2], arows[0:2, j * 128:(j + 1) * 128], ident_f[0:2, 0:2], True, True)
                     for j in range(48)]
            P.op("pe", _mmgroup(items), reads=[Rarows, Rconst], writes=[Rb])
            P.op("act", (lambda bk: lambda e: e.activation(out=adaT, in_=bk[:, 0:96], func=AF.Copy))(bk),
                 reads=[Rb], writes=[RadaT])
            adaT3 = adaT.rearrange("p (j c) -> p j c", c=2)
            g3 = gains.rearrange("p (l s k) -> p l s k", l=2, s=2)
            for s_ in range(2):
                mA, mB = modAB(l, s_)
                sci, shi = 1 + 3 * s_, 3 * s_
                P.op("dve", (lambda mA, sci, gv: lambda e: e.scalar_tensor_tensor(
                    out=mA, in0=adaT3[:, sci * 8:(sci + 1) * 8, 0], scalar=1.0, in1=gv,
                    op0=ALU.add, op1=ALU.mult))(mA, sci, g3[:, l, s_, :]), reads=[RadaT, Rconst], writes=[Rmods])
                P.op("dve", (lambda mB, shi: lambda e: e.tensor_copy(out=mB, in_=adaT3[:, shi * 8:(shi + 1) * 8, 0]))(mB, shi),
                     reads=[RadaT], writes=[Rmods])
            if l == 0:
                P.op("dve", lambda e: e.scalar_tensor_tensor(
                    out=modc[:, 0:8], in0=adaT3[:, 8:16, 1], scalar=1.0, in1=g3[:, 0, 0, :],
                    op0=ALU.add, op1=ALU.mult), reads=[RadaT, Rconst], writes=[Rmods])
                P.op("dve", lambda e: e.tensor_copy(out=modc[:, 8:16], in_=adaT3[:, 0:8, 1]),
                     reads=[RadaT], writes=[Rmods])

        ada_layer(0)
        P.barrier()
        A.off = pmark
        if stage == 0:
            P.replay(block)
            return nc

        def build_G(l, G1, G2, grow, RG, Rgrow):
            for s, G in ((0, G1), (1, G2)):
                base = (2 + 3 * s) * 1024
                for hf in range(2):
                    sp_dma(grow[0:1, :], adarow_s[l, 0:1, base + hf * 512:base + (hf + 1) * 512],
                           reads=[Radarow_s[l]], writes=[Rgrow])
                    bk, Rb = nb()
                    P.op("pe", (lambda bk: lambda e: e.matmul(bk[:, :], ones_f[0:1, 0:128], grow[0:1, :],
                                                              start=True, stop=True))(bk),
                         reads=[Rgrow, Rconst], writes=[Rb])
                    P.op("act", (lambda bk, G, hf: lambda e: e.activation(out=G[:, hf * 512:(hf + 1) * 512], in_=bk[:, :],
                                                                          func=AF.Copy))(bk, G, hf),
                         reads=[Rb], writes=[RG])

        xt = A.alloc(4096, F32); Rxt = Res("xt")
        xt3 = xt.rearrange("p (c d) -> p c d", c=4)
        rc = A.alloc(1024, F32); rs_ = A.alloc(1024, F32); Rrope = Res("rope")
        rc3 = rc.rearrange("p (c d) -> p c d", c=4); rs3 = rs_.rearrange("p (c d) -> p c d", c=4)
        st0 = [A.alloc(512, F32) for _ in range(2)]; st1 = [A.alloc(512, F32) for _ in range(2)]
        st2 = [A.alloc(512, F32) for _ in range(2)]
        Rst0 = [Res("st0_%d" % i) for i in range(2)]; Rst1 = [Res("st1_%d" % i) for i in range(2)]
        Rst2 = [Res("st2_%d" % i) for i in range(2)]
        Lf = A.alloc(4096, F32); Lb = A.alloc(4096, F32); RL = [Res("Lf"), Res("Lb")]
        cst = [A.alloc(512, F32) for _ in range(4)]; Rcst = [Res("cst%d" % i) for i in range(4)]
        xs = A.alloc(4096, BF16); Rxs = Res("xs")
        xs_v = [xs[:, c * 1024:(c + 1) * 1024] for c in range(4)]
        axT = A.alloc(4096, BF16); RaxT = Res("axT"); axT3 = axT.rearrange("p (k t) -> p k t", k=8)
        wring = [A.alloc(4096, BF16) for _ in range(2)]; Rwring = [Res("wring%d" % i) for i in range(2)]
        qr = A.alloc(4096, BF16); kr = A.alloc(4096, BF16); Rqr = Res("qr"); Rkr = Res("kr")
        qr3 = qr.rearrange("p (c d) -> p c d", c=4); kr3 = kr.rearrange("p (c d) -> p c d", c=4)
        ktf = A.alloc(4096, BF16); ktb = A.alloc(4096, BF16); Rkt = [Res("ktf"), Res("ktb")]
        kt3 = [ktf.rearrange("p (c d) -> p c d", c=4), ktb.rearrange("p (c d) -> p c d", c=4)]
        vv = A.alloc(8192, BF16); Rv = Res("v"); v3 = vv.rearrange("p (c d) -> p c d", c=4)
        sgst = [A.alloc(512, BF16) for _ in range(4)]; Rsgst = [Res("sgst%d" % i) for i in range(4)]
        qT = A.alloc(4096, BF16); kT = A.alloc(4096, BF16); RqT = Res("qT"); RkT = Res("kT")
        qT3 = qT.rearrange("p (k t) -> p k t", k=8); kT3 = kT.rearrange("p (k t) -> p k t", k=8)
        RqT_s = [Res("qT_s%d" % i) for i in range(NTILE)]; RkT_s = [Res("kT_s%d" % i) for i in range(NTILE)]
        Rv_s = [Res("v_s%d" % i) for i in range(NTILE)]; Rsg_s = [Res("sg_s%d" % i) for i in range(NTILE)]
        RC_s = [[Res("C_s%d_%d" % (d_, i)) for i in range(NTILE)] for d_ in range(2)]
        Rsctx_s = [Res("sctx_s0"), Res("sctx_s1")]
        Rag_in = [Res("ag_in%d" % k) for k in range(4)]; Rag_out = [Res("ag_out%d" % k) for k in range(4)]
        kdec4 = kdec.rearrange("p (d j h) -> p d j h", d=2, j=4)
        kdecx4 = kdecx.rearrange("p (d j h) -> p d j h", d=2, j=2)
        Lv = [Lf, Lb]
        P.op("dve", lambda e: e.memset(Lf, 0.0), writes=[RL[0]])
        P.op("dve", lambda e: e.memset(Lb, 0.0), writes=[RL[1]])
        P.op("dve", lambda e: e.memset(ssq, 0.0), writes=[Rstat])
        P.op("dve", lambda e: e.memset(ssqo, 0.0), writes=[Rstato])
        cnt = {"w": 0, "st": 0, "sg": 0, "cst": 0}

        def emit_state_gathers():
            for d_ in range(2):
                for hp in range(2):
                    k = d_ * 2 + hp
                    sp_dma(ag_in[k], Lv[d_][:, hp * 2048:(hp + 1) * 2048], reads=[RL[d_]], writes=[Rag_in[k]])
                    P.cc("cc%d" % k, (lambda k: lambda e: e.collective_compute(
                        "AllGather", ALU.bypass, replica_groups=[[0, 1, 2, 3], [4, 5, 6, 7]],
                        ins=[ag_in[k]], outs=[ag_out[k]]))(k), reads=[Rag_in[k]], writes=[Rag_out[k]])

        def phaseA_tile(sc, is_ctx):
            ntc = 2 if is_ctx else 4
            ncols = ntc * 128
            if is_ctx:
                sp_dma(xt3[:, 0:2, :], ctx_d.rearrange("(c p) d -> p c d", p=128), writes=[Rxt])
                mA, mB = modc[:, 0:8], modc[:, 8:16]
                kd = kdecx4
            else:
                sp_dma(xt3, x_d[sc * TT:(sc + 1) * TT, :].rearrange("(c p) d -> p c d", p=128), writes=[Rxt])
                sp_dma(rc3, ropeC_d[:, sc * 4:(sc + 1) * 4, :], writes=[Rrope])
                sp_dma(rs3, ropeS_d[:, sc * 4:(sc + 1) * 4, :], writes=[Rrope])
                mA, mB = modAB(0, 0)
                kd = kdec4
            norm_tile([xt3[:, c, :] for c in range(ntc)], Rxt, 128, xs_v, Rxs, axT3, RaxT, mA, mB)
            blocks = [2, 3, 4, 5, 6, 7] if is_ctx else [2, 3, 4, 5, 6, 7, 0, 1, 8, 9, 10, 11]

            def load_w(blk):
                i = cnt["w"] % 2
                cnt["w"] += 1
                pool_dma(wring[i], wqkvg_d[blk], writes=[Rwring[i]])
                return i

            def transposes(src3, Rsrc, dst3, Rdst):
                for g8 in range(8):
                    bk, Rb = nb()
                    items = [(bk[:, c * 128:(c + 1) * 128], src3[:, c, g8 * 128:(g8 + 1) * 128], ident_b, True, True)
                             for c in range(ntc)]
                    P.op("pe", _mmgroup(items), reads=[Rsrc, Rconst], writes=[Rb])
                    evac_copy(dst3[:, g8, 0:ncols], bk[:, 0:ncols], [Rb], [Rdst])

            nxt = load_w(blocks[0])
            for bi, blk in enumerate(blocks):
                wi = nxt
                if bi + 1 < len(blocks):
                    nxt = load_w(blocks[bi + 1])
                w3 = wring[wi].rearrange("p (k n) -> p k n", k=8)
                for c in range(ntc):
                    bk, Rb = nb()
                    items = [(bk[:, :], axT3[:, kc, c * 128:(c + 1) * 128], w3[:, kc, :], kc == 0, kc == 7)
                             for kc in range(8)]
                    P.op("pe", _mmgroup(items), reads=[RaxT, Rwring[wi]], writes=[Rb])
                    if blk < 4:
                        dst3, Rdst = (qr3, Rqr) if blk < 2 else (kr3, Rkr)
                        hp = blk % 2
                        dsl = dst3[:, c, hp * 512:(hp + 1) * 512]
                        if is_ctx:
                            evac_copy(dsl, bk[:, :], [Rb], [Rdst])
                        else:
                            j = cnt["st"] % 2
                            cnt["st"] += 1
                            P.op("act", (lambda bk, j: lambda e: e.activation(out=st0[j], in_=bk[:, :], func=AF.Copy))(bk, j),
                                 reads=[Rb], writes=[Rst0[j]])
                            s0h = st0[j].rearrange("p (h d) -> p h d", h=2)
                            P.op("dve", (lambda j, c, s0h: lambda e: e.tensor_tensor(
                                out=st1[j].rearrange("p (h d) -> p h d", h=2), in0=s0h,
                                in1=rc3[:, c, :].unsqueeze(1).to_broadcast([128, 2, 256]), op=ALU.mult))(j, c, s0h),
                                reads=[Rst0[j], Rrope], writes=[Rst1[j]])
                            s05 = st0[j].rearrange("p (h a two f) -> p h a two f", h=2, a=2, two=2)
                            t25 = st2[j].rearrange("p (h a two f) -> p h a two f", h=2, a=2, two=2)
                            rs4 = rs3[:, c, :].rearrange("p (a two f) -> p a two f", a=2, two=2)
                            for a_ in range(2):
                                P.op("dve", (lambda a_, s05, t25, rs4: lambda e: e.tensor_tensor(
                                    out=t25[:, :, :, a_, :], in0=s05[:, :, :, 1 - a_, :],
                                    in1=rs4[:, :, a_, :].unsqueeze(1).to_broadcast([128, 2, 2, 64]),
                                    op=ALU.mult))(a_, s05, t25, rs4),
                                    reads=[Rst0[j], Rrope], writes=[Rst2[j]])
                            P.op("dve", (lambda j, dsl: lambda e: e.tensor_tensor(out=dsl, in0=st1[j], in1=st2[j], op=ALU.add))(j, dsl),
                                 reads=[Rst1[j], Rst2[j]], writes=[Rdst])
                    elif blk < 8:
                        evac_copy(v3[:, c, (blk - 4) * 512:(blk - 3) * 512], bk[:, :], [Rb], [Rv])
                    else:
                        j = cnt["sg"] % 4
                        cnt["sg"] += 1
                        P.op("act", (lambda bk, j: lambda e: e.activation(out=sgst[j], in_=bk[:, :], func=AF.Silu))(bk, j),
                             reads=[Rb], writes=[Rsgst[j]])
                        sp_dma(sg_s[sc * 4 + c][:, (blk - 8) * 512:(blk - 7) * 512], sgst[j],
                               reads=[Rsgst[j]], writes=[Rsg_s[sc]])
                if blk == 3:
                    for c in range(ntc):
                        for d_ in range(2):
                            P.op("dve", (lambda c, d_: lambda e: e.tensor_tensor(
                                out=kt3[d_][:, c, :].rearrange("p (h d) -> p h d", h=4),
                                in0=kr3[:, c, :].rearrange("p (h d) -> p h d", h=4),
                                in1=kd[:, d_, c, :].unsqueeze(2).to_broadcast([128, 4, 256]), op=ALU.mult))(c, d_),
                                reads=[Rkr, Rconst], writes=[Rkt[d_]])
                    if not is_ctx:
                        transposes(kr3, Rkr, kT3, RkT)
                if blk == 7:
                    for d_ in range(2):
                        for h in range(H):
                            for dch in range(2):
                                bk, Rb = nb()
                                items = [(bk[:, :], kt3[d_][:, c, h * 256 + dch * 128:h * 256 + (dch + 1) * 128],
                                          v3[:, c, h * 512:(h + 1) * 512], c == 0, c == ntc - 1) for c in range(ntc)]
                                P.op("pe", _mmgroup(items), reads=[Rkt[d_], Rv], writes=[Rb])
                                j = cnt["cst"] % 4
                                cnt["cst"] += 1
                                sl = slice((h * 2 + dch) * 512, (h * 2 + dch + 1) * 512)
                                P.op("act", (lambda bk, j: lambda e: e.activation(out=cst[j], in_=bk[:, :], func=AF.Copy))(bk, j),
                                     reads=[Rb], writes=[Rcst[j]])
                                if is_ctx:
                                    sp_dma(sctx_s[d_][:, sl], cst[j], reads=[Rcst[j]], writes=[Rsctx_s[d_]])
                                else:
                                    sp_dma(C_s[d_, sc][:, sl], cst[j], reads=[Rcst[j]], writes=[RC_s[d_][sc]])
                                    if d_ == 0:
                                        P.op("dve", (lambda bk, sl, h: lambda e: e.scalar_tensor_tensor(
                                            out=Lf[:, sl], in0=Lf[:, sl], scalar=g512f[h], in1=bk[:, :],
                                            op0=ALU.mult, op1=ALU.add))(bk, sl, h), reads=[Rb, Rcst[j]], writes=[RL[0]])
                                    else:
                                        cb = float(np.exp(512.0 * sc * lg_b[h]))
                                        P.op("dve", (lambda bk, sl, cb: lambda e: e.scalar_tensor_tensor(
                                            out=Lb[:, sl], in0=bk[:, :], scalar=cb, in1=Lb[:, sl],
                                            op0=ALU.mult, op1=ALU.add))(bk, sl, cb), reads=[Rb, Rcst[j]], writes=[RL[1]])
                    if (not is_ctx) and sc == NTILE - 1:
                        emit_state_gathers()
                if blk == 1:
                    transposes(qr3, Rqr, qT3, RqT)
            if not is_ctx:
                sp_dma(qT_s[sc], qT, reads=[RqT], writes=[RqT_s[sc]])
                sp_dma(kT_s[sc], kT, reads=[RkT], writes=[RkT_s[sc]])
                sp_dma(v_s[sc], vv, reads=[Rv], writes=[Rv_s[sc]])

        phaseA_tile(0, True)
        for sc in range(NTILE):
            phaseA_tile(sc, False)
        P.barrier()
        A.off = pmark
        if stage == 1:
            P.replay(block)
            return nc

        ada_layer(1)
        pstage = [A.alloc(4096, F32) for _ in range(2)]; Rpst = [Res("pst%d" % i) for i in range(2)]
        Sin = [A.alloc(4096, F32) for _ in range(2)]; RSin = [Res("Sin0"), Res("Sin1")]
        Sbf = [A.alloc(4096, BF16) for _ in range(2)]; RSbf = [Res("Sbf0"), Res("Sbf1")]
        RS_s = [[Res("S_s%d_%d" % (d_, i)) for i in range(NTILE)] for d_ in range(2)]
        pc = {"st": 0, "bf": 0}
        for d_ in range(2):
            for term in range(5):
                j = pc["st"] % 2
                pc["st"] += 1
                if term < 4:
                    for hp in range(2):
                        k = d_ * 2 + hp
                        sp_dma(pstage[j][:, hp * 2048:(hp + 1) * 2048], ag_out[k][term * 128:(term + 1) * 128, :],
                               reads=[Rag_out[k]], writes=[Rpst[j]])
                else:
                    sp_dma(pstage[j], sctx_s[d_], reads=[Rsctx_s[d_]], writes=[Rpst[j]])
                for h in range(H):
                    ci = d_ * 20 + term * 4 + h
                    hs = slice(h * 1024, (h + 1) * 1024)
                    if term == 0:
                        P.op("dve", (lambda j, hs, ci, d_: lambda e: e.tensor_scalar_mul(
                            out=Sin[d_][:, hs], in0=pstage[j][:, hs], scalar1=ccoef[:, ci:ci + 1]))(j, hs, ci, d_),
                            reads=[Rpst[j], Rconst], writes=[RSin[d_]])
                    else:
                        P.op("dve", (lambda j, hs, ci, d_: lambda e: e.scalar_tensor_tensor(
                            out=Sin[d_][:, hs], in0=pstage[j][:, hs], scalar=ccoef[:, ci:ci + 1], in1=Sin[d_][:, hs],
                            op0=ALU.mult, op1=ALU.add))(j, hs, ci, d_),
                            reads=[Rpst[j], Rconst], writes=[RSin[d_]])
            order = list(range(NTILE)) if d_ == 0 else list(range(NTILE - 1, -1, -1))
            gg = g512f if d_ == 0 else g512b
            for n_, sc in enumerate(order):
                jb = pc["bf"] % 2
                pc["bf"] += 1
                P.op("act", (lambda jb, d_: lambda e: e.activation(out=Sbf[jb], in_=Sin[d_], func=AF.Copy))(jb, d_),
                     reads=[RSin[d_]], writes=[RSbf[jb]])
                sp_dma(S_s[d_, sc], Sbf[jb], reads=[RSbf[jb]], writes=[RS_s[d_][sc]])
                if n_ < NTILE - 1:
                    j = pc["st"] % 2
                    pc["st"] += 1
                    sp_dma(pstage[j], C_s[d_, sc], reads=[RC_s[d_][sc]], writes=[Rpst[j]])
                    for h in range(H):
                        hs = slice(h * 1024, (h + 1) * 1024)
                        P.op("dve", (lambda j, hs, h, d_, gg: lambda e: e.scalar_tensor_tensor(
                            out=Sin[d_][:, hs], in0=Sin[d_][:, hs], scalar=gg[h], in1=pstage[j][:, hs],
                            op0=ALU.mult, op1=ALU.add))(j, hs, h, d_, gg),
                            reads=[Rpst[j]], writes=[RSin[d_]])
        P.barrier()
        A.off = pmark
        if stage == 2:
            P.replay(block)
            return nc

        hx = A.alloc(4096, F32); Rhx = Res("hx"); hx3 = hx.rearrange("p (c d) -> p c d", c=4)
        G1 = A.alloc(1024, F32); G2 = A.alloc(1024, F32); RG = Res("G")
        grow = A.alloc(512, F32); Rgrow = Res("grow")
        evst = [A.alloc(512, F32) for _ in range(2)]; Revst = [Res("evst0"), Res("evst1")]
        s1st = [A.alloc(512, F32) for _ in range(2)]; Rs1st = [Res("s1st0"), Res("s1st1")]
        R1 = A.alloc(11264, BF16)
        R2 = A.alloc(12288, BF16)
        R3 = A.alloc(8192, BF16)
        R4 = A.alloc(8192, BF16)
        sgb = R1[:, 0:8192].rearrange("p (c d) -> p c d", c=4); Rsgb = Res("sgb")
        hT3 = R1.rearrange("p (k t) -> p k t", k=HC); RhT = Res("hT")
        ogT3 = R2[:, 0:8192].rearrange("p (k t) -> p k t", k=16); RogT = Res("ogT")
        ogr = [R2[:, 8192 + i * 2048:8192 + (i + 1) * 2048].rearrange("p (c d) -> p c d", c=4) for i in range(2)]
        Rogr = [Res("ogr0"), Res("ogr1")]
        w2ring = [R2[:, i * 5632:(i + 1) * 5632] for i in range(2)]; Rw2ring = [Res("w2r0"), Res("w2r1")]; Rw2ringb = [Res("w2rb0"), Res("w2rb1")]
        woring = [R3[:, i * 4096:(i + 1) * 4096] for i in range(2)]; Rworing = [Res("wor0"), Res("wor1")]
        w13ring = [R3[:, i * 2048:(i + 1) * 2048] for i in range(4)]; Rw13ring = [Res("w13r%d" % i) for i in range(4)]
        qfb = [R4[:, i * 4096:(i + 1) * 4096] for i in range(2)]; Rqfb = [Res("qfT"), Res("qbT")]
        xs2 = R4[:, 0:4096]; Rxs2 = Res("xs2"); xs2_v = [xs2[:, c * 1024:(c + 1) * 1024] for c in range(4)]
        fxT = R4[:, 4096:8192]; RfxT = Res("fxT"); fxT3 = fxT.rearrange("p (k t) -> p k t", k=8)
        shared_mark = A.off
        mixer_res = [Rsgb, RogT] + Rogr + Rworing + Rqfb
        ffn_res = [RhT] + Rw2ring + Rw2ringb + Rw13ring + [Rxs2, RfxT]
        fcnt = {"w13": 0, "w2": 0, "s1": 0, "ev": 0}

        def resid_update(acc, Racc, G, half):
            for ic in range(4):
                j = fcnt["ev"] % 2
                fcnt["ev"] += 1
                P.op("dve", (lambda ic, j: lambda e: e.tensor_tensor(out=evst[j], in0=acc[ic][:, :],
                                                                     in1=G[:, half * 512:(half + 1) * 512], op=ALU.mult))(ic, j),
                     reads=[Racc[ic], RG], writes=[Revst[j]])
                P.op("dve", (lambda ic, j: lambda e: e.tensor_tensor(out=hx3[:, ic, half * 512:(half + 1) * 512],
                                                                     in0=hx3[:, ic, half * 512:(half + 1) * 512],
                                                                     in1=evst[j], op=ALU.add))(ic, j),
                     reads=[Revst[j]], writes=[Rhx])

        def ffn(l):
            P.handoff(mixer_res, ffn_res)
            mA, mB = modAB(l, 1)
            norm_tile([hx3[:, c, :] for c in range(4)], Rhx, 128, xs2_v, Rxs2, fxT3, RfxT, mA, mB)

            def load13(hc):
                i = fcnt["w13"] % 4
                fcnt["w13"] += 1
                pool_dma(w13ring[i], w13_d[l, hc], writes=[Rw13ring[i]])
                return i
            pend = [load13(0), load13(1)]
            for hc in range(HC):
                wi = pend.pop(0)
                if hc + 2 < HC:
                    pend.append(load13(hc + 2))
                w4 = w13ring[wi].rearrange("p (j k m) -> p j k m", j=2, k=8)
                b1, Rb1 = nb()
                b3, Rb3 = nb()
                P.op("pe", _mmgroup([(b1[:, :], w4[:, 0, kc, :], fxT3[:, kc, :], kc == 0, kc == 7) for kc in range(8)]),
                     reads=[Rw13ring[wi], RfxT], writes=[Rb1])
                P.op("pe", _mmgroup([(b3[:, :], w4[:, 1, kc, :], fxT3[:, kc, :], kc == 0, kc == 7) for kc in range(8)]),
                     reads=[Rw13ring[wi], RfxT], writes=[Rb3])
                j = fcnt["s1"] % 2
                fcnt["s1"] += 1
                P.op("act", (lambda b1, j: lambda e: e.activation(out=s1st[j], in_=b1[:, :], func=AF.Silu))(b1, j),
                     reads=[Rb1], writes=[Rs1st[j]])
                P.op("dve", (lambda b3, j, hc: lambda e: e.tensor_tensor(out=hT3[:, hc, :], in0=b3[:, :], in1=s1st[j],
                                                                         op=ALU.mult))(b3, j, hc),
                     reads=[Rb3, Rs1st[j]], writes=[RhT])

            def load2(half, hcg):
                i = fcnt["w2"] % 2
                fcnt["w2"] += 1
                pool_dma(w2ring[i][:, 0:3072], w2_d[l, half, hcg][:, 0:3072], writes=[Rw2ring[i]])
                pool_dma(w2ring[i][:, 3072:5632], w2_d[l, half, hcg][:, 3072:5632], writes=[Rw2ringb[i]])
                return i
            seq = [(hf, g) for hf in range(2) for g in range(2)]
            pend = [load2(*seq[0])]
            acc = Racc = None
            for n_, (half, hcg) in enumerate(seq):
                wi = pend.pop(0)
                if n_ + 1 < len(seq):
                    pend.append(load2(*seq[n_ + 1]))
                if hcg == 0:
                    pairs = [nb() for _ in range(4)]
                    acc = [p_[0] for p_ in pairs]; Racc = [p_[1] for p_ in pairs]
                w3 = w2ring[wi].rearrange("p (j n) -> p j n", j=11)
                items = []
                for j in range(11):
                    for ic in range(4):
                        items.append((acc[ic][:, :], hT3[:, hcg * 11 + j, ic * 128:(ic + 1) * 128], w3[:, j, :],
                                      hcg == 0 and j == 0, hcg == 1 and j == 10))
                P.op("pe", _mmgroup(items), reads=[RhT, Rw2ring[wi], Rw2ringb[wi]], writes=Racc)
                if hcg == 1:
                    resid_update(acc, Racc, G2, half)
            P.handoff(ffn_res, mixer_res)

        maskT = A.alloc(4 * 896, F32); qdec = A.alloc(4096, F32)
        mask3 = maskT.rearrange("p (h m) -> p h m", h=4)
        qdec4 = qdec.rearrange("p (d h t) -> p d h t", d=2, h=4)
        sp_dma(maskT, maskT_d, writes=[Rconst])
        sp_dma(qdec, qdec_d, writes=[Rconst])
        Sf = A.alloc(4096, BF16); Sb = A.alloc(4096, BF16); RSfb = [Res("Sf"), Res("Sb")]
        Sfb = [Sf, Sb]
        qT_b = A.alloc(4096, BF16); kT_b = A.alloc(4096, BF16); RqTb = Res("qTb"); RkTb = Res("kTb")
        qTb3 = qT_b.rearrange("p (k t) -> p k t", k=8); kTb3 = kT_b.rearrange("p (k t) -> p k t", k=8)
        v_b = A.alloc(8192, BF16); Rvb = Res("vb"); vb3 = v_b.rearrange("p (c d) -> p c d", c=4)
        scm = [A.alloc(2048, BF16).rearrange("p (j t) -> p j t", j=4) for _ in range(2)]
        Rscm = [Res("scm0"), Res("scm1")]
        Rhx0_s = [Res("hx0_s%d" % i) for i in range(NTILE)]
        Rag2_in = Res("ag2_in"); Rag2_out = Res("ag2_out")
        build_G(0, G1, G2, grow, RG, Rgrow)
        bc = {"wo": 0}
        for sc in range(NTILE):
            sp_dma(qT_b, qT_s[sc], reads=[RqT_s[sc]], writes=[RqTb])
            sp_dma(kT_b, kT_s[sc], reads=[RkT_s[sc]], writes=[RkTb])
            sp_dma(v_b, v_s[sc], reads=[Rv_s[sc]], writes=[Rvb])
            for d_ in range(2):
                sp_dma(Sfb[d_], S_s[d_, sc], reads=[RS_s[d_][sc]], writes=[RSfb[d_]])
            for c in range(4):
                sp_dma(sgb[:, c, :], sg_s[sc * 4 + c], reads=[Rsg_s[sc]], writes=[Rsgb])
            sp_dma(hx3, x_d[sc * TT:(sc + 1) * TT, :].rearrange("(c p) d -> p c d", p=128), writes=[Rhx])
            for d_ in range(2):
                P.op("dve", (lambda d_: lambda e: e.tensor_tensor(
                    out=qfb[d_].rearrange("p (h a t) -> p h a t", h=4, a=2),
                    in0=qT_b.rearrange("p (h a t) -> p h a t", h=4, a=2),
                    in1=qdec4[:, d_, :, :].unsqueeze(2).to_broadcast([128, 4, 2, TT]), op=ALU.mult))(d_),
                    reads=[RqTb, Rconst], writes=[Rqfb[d_]])
            qf3 = [qfb[d_].rearrange("p (k t) -> p k t", k=8) for d_ in range(2)]
            for h in range(H):
                sm = scm[h % 2]; Rsm = Rscm[h % 2]
                for jb in range(4):
                    bk, Rb = nb()
                    items = [(bk[:, :], kTb3[:, 2 * h + dch, jb * 128:(jb + 1) * 128], qTb3[:, 2 * h + dch, :],
                              dch == 0, dch == 1) for dch in range(2)]
                    P.op("pe", _mmgroup(items), reads=[RkTb, RqTb], writes=[Rb])
                    P.op("dve", (lambda bk, sm, jb, h: lambda e: e.tensor_tensor(
                        out=sm[:, jb, :], in0=bk[:, :], in1=mask3[:, h, 128 * (3 - jb):128 * (3 - jb) + 512],
                        op=ALU.mult))(bk, sm, jb, h), reads=[Rb, Rconst], writes=[Rsm])
                og = ogr[h % 2]; Rog = Rogr[h % 2]
                for ic in range(4):
                    bk, Rb = nb()
                    items = []
                    for jb in range(4):
                        items.append((bk[:, :], sm[:, jb, ic * 128:(ic + 1) * 128], vb3[:, jb, h * 512:(h + 1) * 512],
                                      jb == 0, False))
                    for d_ in range(2):
                        for dch in range(2):
                            k8 = 2 * h + dch
                            items.append((bk[:, :], qf3[d_][:, k8, ic * 128:(ic + 1) * 128],
                                          Sfb[d_][:, k8 * 512:(k8 + 1) * 512], False, d_ == 1 and dch == 1))
                    P.op("pe", _mmgroup(items), reads=[Rsm, Rvb, Rqfb[0], Rqfb[1], RSfb[0], RSfb[1]], writes=[Rb])
                    P.op("act", (lambda bk: lambda e: e.activation(out=junk[:, 0:512], in_=bk[:, :], func=AF.Square,
                                                                   accum_out=ssqo[:, 0:1]))(bk),
                         reads=[Rb], writes=[Rstato, Rjunk])
                    P.op("act", lambda e: e.activation(out=stdo[:, 0:1], in_=ssqo[:, 0:1], func=AF.Sqrt,
                                                       scale=1.0 / DV, bias=epsc[:, 0:1]),
                         reads=[Rstato, Rconst], writes=[Rstato])
                    P.op("dve", lambda e: e.reciprocal(out=rstdo[:, 0:1], in_=stdo[:, 0:1]), reads=[Rstato], writes=[Rstato])
                    P.op("dve", (lambda bk, og, ic, h: lambda e: e.scalar_tensor_tensor(
                        out=og[:, ic, :], in0=bk[:, :], scalar=rstdo[:, 0:1], in1=sgb[:, ic, h * 512:(h + 1) * 512],
                        op0=ALU.mult, op1=ALU.mult))(bk, og, ic, h),
                        reads=[Rb, Rstato, Rsgb], writes=[Rog])
                for j in range(4):
                    bk, Rb = nb()
                    items = [(bk[:, ic * 128:(ic + 1) * 128], og[:, ic, j * 128:(j + 1) * 128], ident_b, True, True)
                             for ic in range(4)]
                    P.op("pe", _mmgroup(items), reads=[Rog, Rconst], writes=[Rb])
                    evac_copy(ogT3[:, h * 4 + j, :], bk[:, :], [Rb], [RogT])

            def loadwo(half, kcg):
                i = bc["wo"] % 2
                bc["wo"] += 1
                pool_dma(woring[i], wo_d[half, kcg], writes=[Rworing[i]])
                return i
            seq = [(hf, g) for hf in range(2) for g in range(2)]
            pend = [loadwo(*seq[0])]
            acc = Racc = None
            for n_, (half, kcg) in enumerate(seq):
                wi = pend.pop(0)
                if n_ + 1 < len(seq):
                    pend.append(loadwo(*seq[n_ + 1]))
                if kcg == 0:
                    pairs = [nb() for _ in range(4)]
                    acc = [p_[0] for p_ in pairs]; Racc = [p_[1] for p_ in pairs]
                w3 = woring[wi].rearrange("p (k n) -> p k n", k=8)
                items = []
                for k8 in range(8):
                    for ic in range(4):
                        items.append((acc[ic][:, :], ogT3[:, kcg * 8 + k8, ic * 128:(ic + 1) * 128], w3[:, k8, :],
                                      kcg == 0 and k8 == 0, kcg == 1 and k8 == 7))
                P.op("pe", _mmgroup(items), reads=[RogT, Rworing[wi]], writes=Racc)
                if kcg == 1:
                    resid_update(acc, Racc, G1, half)
            if debug:
                sp_dma(dbg2_d[sc * TT:(sc + 1) * TT, :].rearrange("(c p) d -> p c d", p=128), hx3, reads=[Rhx])
            ffn(0)
            sp_dma(hx0_s[sc * TT:(sc + 1) * TT, :].rearrange("(c p) d -> p c d", p=128), hx3,
                   reads=[Rhx], writes=[Rhx0_s[sc]])
            if sc == 0:
                sp_dma(ag2_in[0:1, :], hx3[0:1, 0, :], reads=[Rhx], writes=[Rag2_in])
            if sc == NTILE - 1:
                sp_dma(ag2_in[1:2, :], hx3[127:128, 3, :], reads=[Rhx], writes=[Rag2_in])
        P.cc("cc4", lambda e: e.collective_compute("AllGather", ALU.bypass,
                                                   replica_groups=[[0, 1, 2, 3], [4, 5, 6, 7]],
                                                   ins=[ag2_in], outs=[ag2_out]),
             reads=[Rag2_in], writes=[Rag2_out])
        P.barrier()
        if debug:
            for sc in range(NTILE):
                sp_dma(hx3, hx0_s[sc * TT:(sc + 1) * TT, :].rearrange("(c p) d -> p c d", p=128), reads=[Rhx0_s[sc]], writes=[Rhx])
                sp_dma(dbg_d[sc * TT:(sc + 1) * TT, :].rearrange("(c p) d -> p c d", p=128), hx3, reads=[Rhx])
            P.barrier()
        A.off = shared_mark
        if stage == 3:
            P.replay(block)
            return nc

        build_G(1, G1, G2, grow, RG, Rgrow)
        hrows = A.alloc(1024, F32); hsel = A.alloc(16, F32); ht = A.alloc(1024, F32)
        Rhrows = Res("hrows"); Rht = Res("ht")
        axTh = A.alloc(64, BF16); RaxTh = Res("axTh"); axTh3 = axTh.rearrange("p (k t) -> p k t", k=8)
        xsh = A.alloc(1024, BF16); Rxsh = Res("xsh")
        uh = A.alloc(64, F32); Ruh = Res("uh"); uh3 = uh.rearrange("p (f t) -> p f t", f=8)
        ulast = A.alloc(16, F32); Rulast = Res("ulast")
        cst5 = A.alloc(16, F32); Rcst5 = Res("cst5")
        fing = A.alloc(1024, F32)
        sp_dma(fing, fing_d, writes=[Rconst])
        sp_dma(hsel[0:8, 0:2], hsel_d, writes=[Rconst])
        sp_dma(hrows[0:8, :], ag2_out, reads=[Rag2_out], writes=[Rhrows])
        for hf in range(2):
            bk, Rb = nb()
            P.op("pe", (lambda bk, hf: lambda e: e.matmul(bk[0:2, :], hsel[0:8, 0:2], hrows[0:8, hf * 512:(hf + 1) * 512],
                                                          start=True, stop=True))(bk, hf),
                 reads=[Rhrows, Rconst], writes=[Rb])
            P.op("act", (lambda bk, hf: lambda e: e.activation(out=ht[0:2, hf * 512:(hf + 1) * 512], in_=bk[0:2, :],
                                                               func=AF.Copy))(bk, hf), reads=[Rb], writes=[Rht])
        for i in range(3):
            sp_dma(ht[2 + i:3 + i, :], hx0_s[(i + 1) * TT:(i + 1) * TT + 1, :], reads=[Rhx0_s[i + 1]], writes=[Rht])
        mA1, mB1 = modAB(1, 0)
        norm_tile([ht[0:5, :]], Rht, 5, [xsh], Rxsh, axTh3, RaxTh, mA1, mB1)
        winring = [R3[:, i * 3072:(i + 1) * 3072] for i in range(2)]; Rwinring = [Res("winr0"), Res("winr1")]
        woutring = [R2[:, i * 4096:(i + 1) * 4096] for i in range(2)]; Rwoutring = [Res("woutr0"), Res("woutr1")]
        zT = R1[:, 0:4096]; RzT = Res("zT"); zT3 = zT.rearrange("p (k t) -> p k t", k=8)
        axT1 = R4[:, 0:4096]; RaxT1 = Res("axT1"); axT13 = axT1.rearrange("p (k t) -> p k t", k=8)
        xs1 = R4[:, 4096:8192]; Rxs1 = Res("xs1"); xs1_v = [xs1[:, c * 1024:(c + 1) * 1024] for c in range(4)]
        cstc = [A.alloc(512, F32) for _ in range(2)]; Rcstc = [Res("cstc0"), Res("cstc1")]
        uext = [A.alloc(528, F32) for _ in range(2)]; Ruext = [Res("uext0"), Res("uext1")]
        tcv = [A.alloc(512, F32) for _ in range(2)]; Rtcv = [Res("tcv0"), Res("tcv1")]
        ost = [A.alloc(1024, F32) for _ in range(2)]; Rost = [Res("ost0"), Res("ost1")]
        conv_res = Rwinring + Rwoutring + [RzT, RaxT1, Rxs1]
        lc = {"win": 0, "u": 0, "wout": 0, "o": 0}
        cw3 = convw.rearrange("p (f k) -> p f k", f=8)

        def loadwin(fc):
            i = lc["win"] % 2
            lc["win"] += 1
            pool_dma(winring[i], win_d[fc], writes=[Rwinring[i]])
            return i

        P.handoff(mixer_res + ffn_res, conv_res)
        nxt = loadwin(0)
        for fc in range(8):
            wi = nxt
            if fc + 1 < 8:
                nxt = loadwin(fc + 1)
            w4 = winring[wi].rearrange("p (t k m) -> p t k m", t=3, k=8)
            bk, Rb = nb()
            items = [(bk[:, 0:5], w4[:, 1, kc, :], axTh3[:, kc, 0:5], kc == 0, kc == 7) for kc in range(8)]
            items += [(bk[:, 8:13], w4[:, 2, kc, :], axTh3[:, kc, 0:5], kc == 0, kc == 7) for kc in range(8)]
            P.op("pe", _mmgroup(items), reads=[Rwinring[wi], RaxTh], writes=[Rb])
            P.op("act", (lambda bk: lambda e: e.activation(out=cst5[:, 0:5], in_=bk[:, 0:5], func=AF.Copy))(bk),
                 reads=[Rb], writes=[Rcst5])
            P.op("dve", (lambda bk, fc: lambda e: e.tensor_tensor(out=uh3[:, fc, 0:5], in0=bk[:, 8:13], in1=cst5[:, 0:5],
                                                                  op=ALU.mult))(bk, fc),
                 reads=[Rb, Rcst5], writes=[Ruh])
        for side in range(2):
            P.op("dve", (lambda side: lambda e: e.tensor_scalar_mul(out=uh3[:, :, side], in0=uh3[:, :, side],
                                                                    scalar1=hflag[:, side:side + 1]))(side),
                 reads=[Ruh, Rconst], writes=[Ruh])

        for sc in range(NTILE):
            sp_dma(hx3, hx0_s[sc * TT:(sc + 1) * TT, :].rearrange("(c p) d -> p c d", p=128),
                   reads=[Rhx0_s[sc]], writes=[Rhx])
            norm_tile([hx3[:, c, :] for c in range(4)], Rhx, 128, xs1_v, Rxs1, axT13, RaxT1, mA1, mB1)
            nxt = loadwin(0)
            for fc in range(8):
                wi = nxt
                if fc + 1 < 8:
                    nxt = loadwin(fc + 1)
                w4 = winring[wi].rearrange("p (t k m) -> p t k m", t=3, k=8)
                bks = [nb() for _ in range(3)]
                for t_ in range(3):
                    P.op("pe", _mmgroup([(bks[t_][0][:, :], w4[:, t_, kc, :], axT13[:, kc, :], kc == 0, kc == 7)
                                         for kc in range(8)]),
                         reads=[Rwinring[wi], RaxT1], writes=[bks[t_][1]])
                j = lc["u"] % 2
                lc["u"] += 1
                ue = uext[j]
                P.op("act", (lambda bk, j: lambda e: e.activation(out=cstc[j], in_=bk[:, :], func=AF.Copy))(bks[1][0], j),
                     reads=[bks[1][1]], writes=[Rcstc[j]])
                P.op("dve", (lambda bk, j, ue: lambda e: e.tensor_tensor(out=ue[:, 1:513], in0=bk[:, :], in1=cstc[j],
                                                                         op=ALU.mult))(bks[2][0], j, ue),
                     reads=[bks[2][1], Rcstc[j]], writes=[Ruext[j]])
                if sc == 0:
                    P.op("dve", (lambda ue, fc: lambda e: e.tensor_copy(out=ue[:, 0:1], in_=uh3[:, fc, 0:1]))(ue, fc),
                         reads=[Ruh], writes=[Ruext[j]])
                else:
                    P.op("dve", (lambda ue, fc: lambda e: e.tensor_copy(out=ue[:, 0:1], in_=ulast[:, fc:fc + 1]))(ue, fc),
                         reads=[Rulast], writes=[Ruext[j]])
                rcol = 2 + sc if sc < NTILE - 1 else 1
                P.op("dve", (lambda ue, fc, rcol: lambda e: e.tensor_copy(out=ue[:, 513:514], in_=uh3[:, fc, rcol:rcol + 1]))(ue, fc, rcol),
                     reads=[Ruh], writes=[Ruext[j]])
                P.op("dve", (lambda ue, fc: lambda e: e.tensor_copy(out=ulast[:, fc:fc + 1], in_=ue[:, 512:513]))(ue, fc),
                     reads=[Ruext[j]], writes=[Rulast])
                tv = tcv[j]
                P.op("dve", (lambda ue, tv, fc: lambda e: e.tensor_scalar_mul(out=tv, in0=ue[:, 1:513],
                                                                              scalar1=cw3[:, fc, 1:2]))(ue, tv, fc),
                     reads=[Ruext[j], Rconst], writes=[Rtcv[j]])
                P.op("dve", (lambda ue, tv, fc: lambda e: e.scalar_tensor_tensor(out=tv, in0=ue[:, 0:512], scalar=cw3[:, fc, 0:1],
                                                                                 in1=tv, op0=ALU.mult, op1=ALU.add))(ue, tv, fc),
                     reads=[Ruext[j], Rconst], writes=[Rtcv[j]])
                P.op("dve", (lambda ue, tv, fc: lambda e: e.scalar_tensor_tensor(out=tv, in0=ue[:, 2:514], scalar=cw3[:, fc, 2:3],
                                                                                 in1=tv, op0=ALU.mult, op1=ALU.add))(ue, tv, fc),
                     reads=[Ruext[j], Rconst], writes=[Rtcv[j]])
                P.op("dve", (lambda bk, tv, fc: lambda e: e.tensor_tensor(out=zT3[:, fc, :], in0=bk[:, :], in1=tv,
                                                                          op=ALU.mult))(bks[0][0], tv, fc),
                     reads=[bks[0][1], Rtcv[j]], writes=[RzT])
            for half in range(2):
                i = lc["wout"] % 2
                lc["wout"] += 1
                pool_dma(woutring[i], wout_d[half], writes=[Rwoutring[i]])
                pairs = [nb() for _ in range(4)]
                acc = [p_[0] for p_ in pairs]; Racc = [p_[1] for p_ in pairs]
                w3 = woutring[i].rearrange("p (k n) -> p k n", k=8)
                items = []
                for k8 in range(8):
                    for ic in range(4):
                        items.append((acc[ic][:, :], zT3[:, k8, ic * 128:(ic + 1) * 128], w3[:, k8, :], k8 == 0, k8 == 7))
                P.op("pe", _mmgroup(items), reads=[RzT, Rwoutring[i]], writes=Racc)
                resid_update(acc, Racc, G1, half)
            P.handoff(conv_res, ffn_res + mixer_res)
            ffn(1)
            P.handoff(mixer_res + ffn_res, conv_res)
            for c in range(4):
                P.op("act", (lambda c: lambda e: e.activation(out=junk, in_=hx3[:, c, :], func=AF.Square,
                                                              accum_out=ssq[:, c:c + 1]))(c),
                     reads=[Rhx], writes=[Rstat, Rjunk])
            P.op("act", lambda e: e.activation(out=std[:, 0:4], in_=ssq[:, 0:4], func=AF.Sqrt, scale=1.0 / D,
                                               bias=epsc[:, 0:1]), reads=[Rstat, Rconst], writes=[Rstat])
            P.op("dve", lambda e: e.reciprocal(out=rstd[:, 0:4], in_=std[:, 0:4]), reads=[Rstat], writes=[Rstat])
            for c in range(4):
                j = lc["o"] % 2
                lc["o"] += 1
                P.op("dve", (lambda c, j: lambda e: e.scalar_tensor_tensor(out=ost[j], in0=hx3[:, c, :], scalar=rstd[:, c:c + 1],
                                                                           in1=fing, op0=ALU.mult, op1=ALU.mult))(c, j),
                     reads=[Rhx, Rstat, Rconst], writes=[Rost[j]])
                sp_dma(out_d[sc * TT + c * 128:sc * TT + (c + 1) * 128, :], ost[j], reads=[Rost[j]])
        P.barrier()
        P.replay(block)
    return nc


def _const_tables():
    lg_f = np.log1p(-np.exp2(-5.0 - np.arange(H, dtype=np.float64)))
    lg_b = np.log1p(-np.exp2(-5.5 - np.arange(H, dtype=np.float64)))
    scale = DK ** -0.5
    p = np.arange(128, dtype=np.float64)
    kdec = np.zeros((128, 2, 4, H))
    kdecx = np.zeros((128, 2, 2, H))
    for jb in range(4):
        pos = 128 * jb + p
        kdec[:, 0, jb, :] = scale * np.exp((TT - 1 - pos)[:, None] * lg_f[None, :])
        kdec[:, 1, jb, :] = scale * np.exp(pos[:, None] * lg_b[None, :])
    for jb in range(2):
        pos = 128 * jb + p
        kdecx[:, 0, jb, :] = scale * np.exp((CTX - 1 - pos)[:, None] * lg_f[None, :])
        kdecx[:, 1, jb, :] = scale * np.exp(pos[:, None] * lg_b[None, :])
    m = np.arange(896, dtype=np.float64)
    rel = m[None, :] - 384.0 - p[:, None]
    maskT = np.zeros((128, H, 896))
    for h in range(H):
        mf = np.exp(np.maximum(rel, 0.0) * lg_f[h])
        mb = np.exp(np.maximum(-rel, 0.0) * lg_b[h])
        maskT[:, h, :] = scale * np.where(rel > 0, mf, np.where(rel < 0, mb, 2.0))
    i = np.arange(TT, dtype=np.float64)
    qdec = np.zeros((128, 2, H, TT))
    for h in range(H):
        qdec[:, 0, h, :] = np.exp((i + 1.0) * lg_f[h])[None, :]
        qdec[:, 1, h, :] = np.exp((TT - i) * lg_b[h])[None, :]
    return lg_f, lg_b, kdec, kdecx, maskT, qdec


def _core_tables(j, lg_f, lg_b):
    cc = np.zeros((2, 5, H))
    for i in range(4):
        if i < j:
            cc[0, i, :] = np.exp(float(NTOK) * (j - 1 - i) * lg_f)
        if i > j:
            cc[1, i, :] = np.exp(float(NTOK) * (i - j - 1) * lg_b)
    cc[0, 4, :] = np.exp(float(NTOK) * j * lg_f)
    cc[1, 4, :] = np.exp(float(NTOK) * (3 - j) * lg_b)
    ccoef = np.broadcast_to(cc.reshape(1, 40), (128, 40)).astype(np.float32)
    ccoef = np.where(np.abs(ccoef) < 1e-37, 0.0, ccoef).astype(np.float32)
    hsel = np.zeros((8, 2), np.float32)
    hflag = np.zeros((128, 2), np.float32)
    if j > 0:
        hsel[2 * (j - 1) + 1, 0] = 1.0
        hflag[:, 0] = 1.0
    if j < 3:
        hsel[2 * (j + 1), 1] = 1.0
        hflag[:, 1] = 1.0
    t = (j * NTOK + np.arange(NTOK)).astype(np.int64)
    row = (t // GRID_W).astype(np.float32)
    col = (t % GRID_W).astype(np.float32)
    freqs = (10000.0 ** (-np.arange(64, dtype=np.float32) / 64)).astype(np.float32)
    C = np.zeros((NTOK, 256), np.float32)
    S = np.zeros((NTOK, 256), np.float32)
    for half, pos in ((0, row), (1, col)):
        ang = (pos[:, None] * freqs[None, :]).astype(np.float32)
        cs, sn = np.cos(ang).astype(np.float32), np.sin(ang).astype(np.float32)
        b0 = half * 128
        C[:, b0:b0 + 64] = cs
        C[:, b0 + 64:b0 + 128] = cs
        S[:, b0:b0 + 64] = -sn
        S[:, b0 + 64:b0 + 128] = sn
    ropeC = np.ascontiguousarray(C.reshape(16, 128, 256).transpose(1, 0, 2))
    ropeS = np.ascontiguousarray(S.reshape(16, 128, 256).transpose(1, 0, 2))
    return ccoef, hsel, hflag, ropeC, ropeS


def _prep_shared(inp):
    f = lambda a: np.ascontiguousarray(a, dtype=np.float32)
    sh = {}
    sh["ada_w"] = f(inp["ada_w"].reshape(2, 8, 128, 12, 512).transpose(0, 3, 2, 1, 4)).reshape(2, 12, 128, 4096)
    sh["ada_b"] = f(inp["ada_b"].reshape(2, 1, 6144))
    g = np.zeros((128, 2, 2, 8), np.float32)
    for l in range(2):
        g[:, l, 0, :] = inp["norm_mix"][l].reshape(8, 128).T
        g[:, l, 1, :] = inp["norm_ffn"][l].reshape(8, 128).T
    sh["gains"] = f(g.reshape(128, 32))
    sh["fin_g"] = f(np.broadcast_to(inp["final_norm"].reshape(1, D), (128, D)))
    W = inp["ret_w_qkvg"][0]
    sh["wqkvg"] = f(W.reshape(8, 128, 12, 512).transpose(2, 1, 0, 3)).reshape(12, 128, 4096)
    Wo = inp["ret_w_o"][0]
    sh["wo"] = f(Wo.reshape(2, 8, 128, 2, 512).transpose(3, 0, 2, 1, 4)).reshape(2, 2, 128, 4096)
    w13 = np.zeros((2, HC, 128, 2, 8, 128), np.float32)
    for l in range(2):
        w13[l, :, :, 0] = inp["ffn_w1"][l].reshape(8, 128, HC, 128).transpose(2, 1, 0, 3)
        w13[l, :, :, 1] = inp["ffn_w3"][l].reshape(8, 128, HC, 128).transpose(2, 1, 0, 3)
    sh["w13"] = w13.reshape(2, HC, 128, 2048)
    w2 = np.stack([inp["ffn_w2"][l].reshape(2, 11, 128, 2, 512).transpose(3, 0, 2, 1, 4) for l in range(2)])
    sh["w2"] = f(w2).reshape(2, 2, 2, 128, 11 * 512)
    Win = inp["conv_w_in"][0]
    sh["win"] = f(Win.reshape(8, 128, 3, 8, 128).transpose(3, 1, 2, 0, 4)).reshape(8, 128, 3072)
    Wout = inp["conv_w_out"][0]
    sh["wout"] = f(Wout.reshape(8, 128, 2, 512).transpose(2, 1, 0, 3)).reshape(2, 128, 4096)
    sh["convw"] = f(inp["conv_w"][0].reshape(3, 8, 128).transpose(2, 1, 0)).reshape(128, 24)
    lg_f, lg_b, kdec, kdecx, maskT, qdec = _const_tables()
    sh["kdec"] = f(kdec.reshape(128, 32))
    sh["kdecx"] = f(kdecx.reshape(128, 16))
    sh["maskT"] = f(maskT.reshape(128, 4 * 896))
    sh["qdec"] = f(qdec.reshape(128, 2 * 4 * 512))
    sh["ident"] = np.eye(128, dtype=np.float32)
    return sh, lg_f, lg_b


_NC_CACHE = {}


def kernel(**inputs):
    inp = {k: np.asarray(v) for k, v in inputs.items()}
    debug = bool(inp.pop("_debug", False))
    stage = int(inp.pop("_stage", 9))
    sh, lg_f, lg_b = _prep_shared(inp)
    in_maps = []
    for core in range(NCORE):
        b, j = core // 4, core % 4
        m = dict(sh)
        m["x"] = np.ascontiguousarray(inp["x"][b, j * NTOK:(j + 1) * NTOK, :], dtype=np.float32)
        m["ctxb"] = np.ascontiguousarray(inp["ctx"][b], dtype=np.float32)
        cT = np.zeros((128, 8, 2), np.float32)
        cT[:, :, 0] = inp["c"][b].reshape(8, 128).T
        cT[:, :, 1] = inp["c_ctx"].reshape(8, 128).T
        m["cT"] = cT.reshape(128, 16)
        ccoef, hsel, hflag, ropeC, ropeS = _core_tables(j, lg_f, lg_b)
        m["ccoef"], m["hsel"], m["hflag"], m["ropeC"], m["ropeS"] = ccoef, hsel, hflag, ropeC, ropeS
        in_maps.append(m)
    if (debug, stage) not in _NC_CACHE:
        _NC_CACHE[(debug, stage)] = build_nc(debug, stage)
    nc = _NC_CACHE[(debug, stage)]
    res = run_bass_kernel_spmd(nc, in_maps, core_ids=list(range(NCORE)))
    out = np.zeros((2, SEQ, D), np.float32)
    for core in range(NCORE):
        b, j = core // 4, core % 4
        out[b, j * NTOK:(j + 1) * NTOK, :] = res.results[core]["out"]
    if debug:
        dbg = np.zeros((2, SEQ, D), np.float32)
        dbg2 = np.zeros((2, SEQ, D), np.float32)
        for core in range(NCORE):
            b, j = core // 4, core % 4
            dbg[b, j * NTOK:(j + 1) * NTOK, :] = res.results[core]["dbg"]
            dbg2[b, j * NTOK:(j + 1) * NTOK, :] = res.results[core]["dbg2"]
        dbg3 = np.stack([res.results[core]["dbg3"] for core in range(NCORE)])
        return out, dbg, dbg2, dbg3
    return out
```
